# Optimizing a Trainium2 kernel written in Bass

```python
import math
import jax
import jax.numpy as jnp
from jax import lax
import numpy as np

D_MODEL = 4096
BATCH = 4
SEQ = 2048
DEPTH = 1
DEC_BATCH = 128
DEC_SEQ = 4
PAST_LEN = 16384
PAGE_SIZE = 128

SSM_WIDTH = D_MODEL // 2
SSM_GROUP = 16
SSM_GROUPS = SSM_WIDTH // SSM_GROUP
SSM_STATE = 64
DT_MIN = 1e-3
DT_MAX = 1e-1
CONV_WIDTH = D_MODEL // 2
CONV_K = 3
IN_WIDTHS = (SSM_WIDTH, SSM_WIDTH, CONV_WIDTH, CONV_WIDTH, CONV_WIDTH, CONV_WIDTH, D_MODEL, D_MODEL)
IN_COLS = 2 * SSM_WIDTH + 4 * CONV_WIDTH + 2 * D_MODEL
EPS = 1e-6

kernel_name = "hybrid_s5_shortconv_gated_decoder_step"


def _rmsnorm(x, g):
    xf = x.astype(jnp.float32)
    y = xf * lax.rsqrt(jnp.mean(xf * xf, axis=-1, keepdims=True) + EPS)
    return (y * g.astype(jnp.float32)).astype(x.dtype)


def _s5(u, h0, lam_re, lam_im, log_dt, b_re, b_im, c_re, c_im, d_skip):
    n, l, _ = u.shape
    f32 = jnp.float32
    uf = u.astype(f32).reshape(n, l, SSM_GROUPS, SSM_GROUP)
    lam = lax.complex(lam_re.astype(f32), lam_im.astype(f32))
    dt = jnp.exp(log_dt.astype(f32))[:, None]
    lam_bar = jnp.exp(lam * dt)
    b = lax.complex(b_re.astype(f32), b_im.astype(f32))
    b_bar = ((lam_bar - 1.0) / lam)[..., None] * b
    bu = jnp.einsum('gph,nlgh->nlgp', b_bar, uf.astype(jnp.complex64))
    a = jnp.broadcast_to(lam_bar, bu.shape)

    def combine(e1, e2):
        a1, b1 = e1
        a2, b2 = e2
        return a1 * a2, a2 * b1 + b2

    a_cum, h = lax.associative_scan(combine, (a, bu), axis=1)
    h0c = lax.complex(h0[..., 0].astype(f32), h0[..., 1].astype(f32))
    h = h + a_cum * h0c[:, None]
    c = lax.complex(c_re.astype(f32), c_im.astype(f32))
    y = jnp.einsum('ghp,nlgp->nlgh', c, h).real \
        + d_skip.astype(f32).reshape(SSM_GROUPS, SSM_GROUP) * uf
    h_last = h[:, -1]
    new_state = jnp.stack([h_last.real, h_last.imag], axis=-1)
    return y.reshape(n, l, SSM_WIDTH).astype(u.dtype), new_state


def _short_conv(v, buf, w):
    l = v.shape[1]
    full = jnp.concatenate([buf.astype(v.dtype), v], axis=1)
    out = full[:, 0:l] * w[0]
    for k in range(1, CONV_K):
        out = out + full[:, k:k + l] * w[k]
    return out, full[:, -(CONV_K - 1):]


def _layer(x, c, ssm_h0, conv_buf, norm_g, w_ada, b_ada, w_in, lam_re, lam_im, log_dt,
           b_re, b_im, c_re, c_im, d_skip, w_glu, b_glu, w_pa, conv_w, w_pb, w_o):
    mod = jax.nn.silu(c) @ w_ada + b_ada
    shift, scale, gate = jnp.split(mod, 3, axis=-1)
    xn = _rmsnorm(x, norm_g) * (1.0 + scale[:, None]) + shift[:, None]
    proj = xn @ w_in
    split_points = np.cumsum(IN_WIDTHS)[:-1].tolist()
    u_a, z_a, h_b, c_b, b_b, z_b, g_a, g_b = jnp.split(proj, split_points, axis=-1)
    y_a, ssm_new = _s5(u_a, ssm_h0, lam_re, lam_im, log_dt, b_re, b_im, c_re, c_im, d_skip)
    y_a = jax.nn.gelu(y_a)
    y_a = y_a * jax.nn.sigmoid(y_a @ w_glu + b_glu)
    branch_a = (y_a * jax.nn.silu(z_a)) @ w_pa
    conv_out, conv_new = _short_conv(c_b * h_b, conv_buf, conv_w)
    branch_b = (b_b * conv_out * jax.nn.silu(z_b)) @ w_pb
    merged = jax.nn.sigmoid(g_a) * branch_a + jax.nn.sigmoid(g_b) * branch_b
    out = merged @ w_o
    return x + gate[:, None] * out, ssm_new, conv_new


def setup_inputs(seed: int = 0) -> dict:
    key = jax.random.key(seed)
    ks = jax.random.split(key, 32)
    f32 = jnp.float32
    nrm = lambda k, shape, s: (jax.random.normal(k, shape, f32) * s)
    lam_im_base = math.pi * jnp.arange(SSM_STATE, dtype=f32)
    return {
        "x_prompt": nrm(ks[0], (BATCH, SEQ, D_MODEL), 1.0),
        "x_sample": nrm(ks[1], (DEC_BATCH, DEC_SEQ, D_MODEL), 1.0),
        "state_ssm": nrm(ks[2], (DEPTH, DEC_BATCH, SSM_GROUPS, SSM_STATE, 2), 0.1),
        "state_conv": nrm(ks[3], (DEPTH, DEC_BATCH, CONV_K - 1, CONV_WIDTH), 1.0),
        "c_prompt": nrm(ks[4], (BATCH, D_MODEL), 1.0),
        "c_sample": nrm(ks[5], (DEC_BATCH, D_MODEL), 1.0),
        "norm_g": 1.0 + nrm(ks[6], (DEPTH, D_MODEL), 0.02),
        "w_ada": nrm(ks[7], (DEPTH, D_MODEL, 3 * D_MODEL), 0.5 * D_MODEL ** -0.5),
        "b_ada": nrm(ks[8], (DEPTH, 3 * D_MODEL), 0.02),
        "w_in": nrm(ks[9], (DEPTH, D_MODEL, IN_COLS), D_MODEL ** -0.5),
        "lam_re": -0.5 + nrm(ks[10], (DEPTH, SSM_GROUPS, SSM_STATE), 0.01),
        "lam_im": lam_im_base + nrm(ks[11], (DEPTH, SSM_GROUPS, SSM_STATE), 0.01),
        "log_dt": jax.random.uniform(ks[12], (DEPTH, SSM_GROUPS), f32,
                                     minval=math.log(DT_MIN), maxval=math.log(DT_MAX)),
        "b_re": nrm(ks[13], (DEPTH, SSM_GROUPS, SSM_STATE, SSM_GROUP), (2 * SSM_GROUP) ** -0.5),
        "b_im": nrm(ks[14], (DEPTH, SSM_GROUPS, SSM_STATE, SSM_GROUP), (2 * SSM_GROUP) ** -0.5),
        "c_re": nrm(ks[15], (DEPTH, SSM_GROUPS, SSM_GROUP, SSM_STATE), 0.5),
        "c_im": nrm(ks[16], (DEPTH, SSM_GROUPS, SSM_GROUP, SSM_STATE), 0.5),
        "d_skip": nrm(ks[17], (DEPTH, SSM_WIDTH), 1.0),
        "w_glu": nrm(ks[18], (DEPTH, SSM_WIDTH, SSM_WIDTH), SSM_WIDTH ** -0.5),
        "b_glu": nrm(ks[19], (DEPTH, SSM_WIDTH), 0.02),
        "w_pa": nrm(ks[20], (DEPTH, SSM_WIDTH, D_MODEL), SSM_WIDTH ** -0.5),
        "conv_w": nrm(ks[21], (DEPTH, CONV_K, CONV_WIDTH), CONV_K ** -0.5),
        "w_pb": nrm(ks[22], (DEPTH, CONV_WIDTH, D_MODEL), CONV_WIDTH ** -0.5),
        "w_o": nrm(ks[23], (DEPTH, D_MODEL, D_MODEL), D_MODEL ** -0.5),
        "final_g": 1.0 + nrm(ks[24], (D_MODEL,), 0.02),
    }


def reference(x_prompt, x_sample, state_ssm, state_conv, c_prompt, c_sample, norm_g, w_ada, b_ada,
              w_in, lam_re, lam_im, log_dt, b_re, b_im, c_re, c_im, d_skip, w_glu, b_glu, w_pa,
              conv_w, w_pb, w_o, final_g):
    n_p = x_prompt.shape[0]
    ssm_p0 = jnp.zeros((n_p, SSM_GROUPS, SSM_STATE, 2), jnp.float32)
    conv_p0 = jnp.zeros((n_p, CONV_K - 1, CONV_WIDTH), x_prompt.dtype)
    hp, hs = x_prompt, x_sample
    ssm_p_all, conv_p_all, ssm_s_all, conv_s_all = [], [], [], []
    for l in range(DEPTH):
        params = (norm_g[l], w_ada[l], b_ada[l], w_in[l], lam_re[l], lam_im[l], log_dt[l],
                  b_re[l], b_im[l], c_re[l], c_im[l], d_skip[l], w_glu[l], b_glu[l], w_pa[l],
                  conv_w[l], w_pb[l], w_o[l])
        hp, sp, cp = _layer(hp, c_prompt, ssm_p0, conv_p0, *params)
        hs, ss, cs = _layer(hs, c_sample, state_ssm[l], state_conv[l], *params)
        ssm_p_all.append(sp)
        conv_p_all.append(cp)
        ssm_s_all.append(ss)
        conv_s_all.append(cs)
    y_prompt = _rmsnorm(hp, final_g)
    y_sample = _rmsnorm(hs, final_g)
    return (y_prompt, y_sample, jnp.stack(ssm_p_all), jnp.stack(conv_p_all),
            jnp.stack(ssm_s_all), jnp.stack(conv_s_all))
```

```python
import contextlib
import math
import numpy as np
import concourse.bass as bass
import concourse.mybir as mybir
from concourse.bass_utils import run_bass_kernel_spmd

F32 = mybir.dt.float32
BF16 = mybir.dt.bfloat16
ALU = mybir.AluOpType
AF = mybir.ActivationFunctionType

D = 4096
NDS = 48
ARENA_BYTES = 188 * 1024


class Buf:
    def __init__(self):
        self.w = []
        self.r = []


class K:
    def __init__(self, nc, es):
        self.nc = nc
        self.E = {"pe": nc.tensor, "act": nc.scalar, "dve": nc.vector, "pool": nc.gpsimd, "sp": nc.sync}
        self.sem = {e: es.enter_context(nc.semaphore("sem_" + e)) for e in self.E}
        self.cnt = {e: 0 for e in self.E}
        self.waited = {e: {} for e in self.E}
        self.dsem = [es.enter_context(nc.semaphore("dsem%d" % i)) for i in range(NDS)]
        self.dcount = [0] * NDS
        self.dtok = [None] * NDS
        self.dnext = 0
        self.dnext2 = {"sp": 0, "pool": 0}

    def wait(self, e, tok):
        if tok is None:
            return
        if tok[0] == "e":
            if e == "pe" and tok[1] == "pe":
                return
            key = tok[1]
            if self.waited[e].get(key, 0) >= tok[2]:
                return
            self.E[e].wait_ge(self.sem[tok[1]], tok[2])
            self.waited[e][key] = tok[2]
        else:
            key = ("d", tok[1])
            if self.waited[e].get(key, 0) >= tok[2]:
                return
            self.E[e].wait_ge(self.dsem[tok[1]], tok[2])
            self.waited[e][key] = tok[2]

    def begin(self, e, reads=(), writes=(), par=False):
        for b in reads:
            for t in b.w:
                self.wait(e, t)
        for b in writes:
            if not par:
                for t in b.w:
                    self.wait(e, t)
            for t in b.r:
                self.wait(e, t)

    def _upd(self, e, tok, reads, writes, append=False, par=False):
        for b in writes:
            if par:
                b.w = [t for t in b.w if not (t[0] == "e" and tok[0] == "e" and t[1] == tok[1])] + [tok]
            elif append:
                b.w = b.w + [tok]
                b.r = []
            else:
                b.w = [tok]
                b.r = []
        for b in reads:
            if b in writes:
                continue
            if tok[0] == "e":
                b.r = [t for t in b.r if not (t[0] == "e" and t[1] == e)]
            b.r.append(tok)

    def end(self, e, ins, reads=(), writes=(), par=False):
        ins.then_inc(self.sem[e], 1)
        self.cnt[e] += 1
        tok = ("e", e, self.cnt[e])
        self._upd(e, tok, reads, writes, par=par)
        return tok

    def op(self, e, fn, reads=(), writes=(), par=False):
        self.begin(e, reads, writes, par=par)
        ins = fn(self.E[e])
        return self.end(e, ins, reads, writes, par=par)

    def dma(self, e, out, in_, reads=(), writes=(), append=False, **kw):
        if not append:
            self.begin(e, reads, writes)
        half = NDS // 2
        base = 0 if e == "sp" else half
        i = base + self.dnext2[e]
        self.dnext2[e] = (self.dnext2[e] + 1) % half
        self.wait(e, self.dtok[i])
        ins = self.E[e].dma_start(out=out, in_=in_, **kw)
        self.dcount[i] += 16
        ins.then_inc(self.dsem[i], 16)
        tok = ("d", i, self.dcount[i])
        self.dtok[i] = tok
        self._upd(e, tok, reads, writes, append)
        return tok

    def barrier(self):
        for e in self.E:
            for e2 in self.E:
                if e2 != e and self.cnt[e2] > 0:
                    self.wait(e, ("e", e2, self.cnt[e2]))
            if e == "pe":
                pass
            for t in self.dtok:
                self.wait(e, t)


def build_nc():
    nc = bass.Bass("TRN2", target_bir_lowering=False)
    di = lambda n, s, dt=F32: nc.dram_tensor(n, s, dt, kind="ExternalInput").ap()
    do = lambda n, s: nc.dram_tensor(n, s, F32, kind="ExternalOutput").ap()
    dint = lambda n, s, dt: nc.dram_tensor(n, s, dt, kind="Internal").ap()
    xm = di("xm", [1024, D]); xp = di("xp", [1024, D]); xs = di("xs", [64, D])
    cp = di("cp", [1, D]); cs = di("cs", [16, D])
    sssm = di("sssm", [16, 128, 64, 2]); sconv = di("sconv", [16, 2, 2048]); flag = di("flag", [128, 1])
    ident = di("ident", [128, 128]); maskd = di("mask", [128, 128])
    norm_g = di("norm_g", [D]); w_ada = di("w_ada", [D, 3 * D]); b_ada = di("b_ada", [3 * D])
    w_in = di("w_in", [D, 20480])
    lam_re = di("lam_re", [128, 64]); lam_im = di("lam_im", [128, 64]); log_dt = di("log_dt", [128])
    b_re = di("b_re", [128, 64, 16]); b_im = di("b_im", [128, 64, 16])
    c_re = di("c_re", [128, 16, 64]); c_im = di("c_im", [128, 16, 64])
    d_skip = di("d_skip", [2048]); w_glu = di("w_glu", [2048, 2048]); b_glu = di("b_glu", [2048])
    w_pa = di("w_pa", [2048, D]); conv_w = di("conv_w", [3, 2048]); w_pb = di("w_pb", [2048, D])
    w_o = di("w_o", [D, D]); final_g = di("final_g", [D])
    yp = do("yp", [1024, D]); ys = do("ys", [64, D]); nsp = do("nsp", [128, 64, 2]); ncp = do("ncp", [2, 2048])
    nss = do("nss", [16, 128, 64, 2]); ncs = do("ncs", [16, 2, 2048])
    WSTR = dint("WSTR", [128, 128, 64], BF16); WSTI = dint("WSTI", [128, 128, 64], BF16)
    TOEP = dint("TOEP", [128, 128, 128], BF16)
    WOR = dint("WOR", [64, 128, 128], BF16); WOI = dint("WOI", [64, 128, 128], BF16)
    GROWD = dint("GROWD", [33, 4096], F32)

    es = contextlib.ExitStack()
    with es:
        ARENA = es.enter_context(nc.sbuf_tensor("arena", [128, ARENA_BYTES // 2], BF16))
        PS = [es.enter_context(nc.psum_tensor("ps%d" % i, [128, 512], F32)) for i in range(8)]
        PSB = [Buf() for _ in range(8)]
        k = K(nc, es)
        psn = [0]

        def nextps():
            i = psn[0]
            psn[0] = (i + 1) % 8
            return i

        def V(off, parts, shape, dt=F32):
            n = 1
            for s in shape:
                n *= s
            sz = 2 if dt == BF16 else 4
            assert off % 4 == 0 and off + n * sz <= ARENA_BYTES, (off, shape)
            a = ARENA[0:parts, off // 2: off // 2 + n * sz // 2]
            if dt != BF16:
                a = a.bitcast(dt)
            if len(shape) > 1:
                names = ["d%d" % i for i in range(len(shape))]
                kw = {names[i]: shape[i] for i in range(1, len(shape))}
                a = a.rearrange("p (" + " ".join(names) + ") -> p " + " ".join(names), **kw)
            return a

        IDB = V(0, 128, [128], BF16); IDF = V(256, 128, [128]); MASK = V(768, 128, [128])
        A8R = V(1280, 64, [128]); A8I = V(1792, 64, [128]); A4R = V(2304, 64, [128]); A4I = V(2816, 64, [128])
        HST = [[V(3328, 64, [128]), V(3840, 64, [128])], [V(4352, 64, [128]), V(4864, 64, [128])]]
        GSP = V(5376, 128, [32]); SHP = V(5504, 128, [32]); NG = V(5632, 128, [32])
        BGLU = V(5760, 128, [16]); CW = V(5824, 128, [3, 16]); PREVP = V(6016, 128, [16, 2])
        FLAG = V(6144, 128, [1]); SS = V(6148, 128, [1]); RS = V(6152, 128, [1]); ONES = V(6160, 128, [128])
        GSS = V(6672, 128, [32, 16]); SHS = V(8720, 128, [32, 16]); PREVS = V(10768, 128, [16, 16, 2])
        CONVS = V(12816, 128, [16, 16, 2]); XNH = V(14864, 128, [32, 2], BF16)
        XN_S = V(15040, 128, [32, 64], BF16); UA_S = V(19136, 128, [16, 64], BF16); ZSA_S = V(21184, 128, [16, 64], BF16)
        VB_S = V(23232, 128, [16, 64], BF16); MERGED_S = V(25280, 128, [32, 64], BF16)
        MGS_FLAT = V(25280, 128, [33 * 64], BF16)
        STMP = 29504
        U0 = 32768
        b_const = Buf(); b_mod = Buf(); b_hst = Buf(); b_prev = Buf(); b_small = Buf(); b_grow = Buf()

        def dv(fn, reads=(), writes=(), e="dve", par=False):
            return k.op(e, fn, reads, writes, par=par)

        k.dma("sp", IDF, ident, writes=[b_const])
        k.dma("sp", MASK, maskd, writes=[b_const], append=True)
        k.dma("sp", FLAG, flag, writes=[b_const], append=True)
        k.dma("sp", NG, norm_g.rearrange("(k p) -> p k", p=128), writes=[b_const], append=True, allow_slow_non_contiguous=True)
        k.dma("sp", BGLU, b_glu.rearrange("(k p) -> p k", p=128), writes=[b_const], append=True, allow_slow_non_contiguous=True)
        k.dma("sp", CW, conv_w.rearrange("r (k p) -> p r k", p=128), writes=[b_const], append=True, allow_slow_non_contiguous=True)
        dv(lambda e: e.tensor_copy(out=IDB, in_=IDF), reads=[b_const], writes=[b_const])
        dv(lambda e: e.memset(ONES, 1.0), writes=[b_const])
        for st in (HST[0][0], HST[0][1], HST[1][0], HST[1][1]):
            dv(lambda e, st=st: e.memset(st, 0.0), writes=[b_hst])
        dv(lambda e: e.memset(PREVP, 0.0), writes=[b_prev])
        dv(lambda e: e.memset(MGS_FLAT[:, 32 * 64:33 * 64], 0.0), writes=[b_const])

        RING_OFF = [U0 + 112 * 1024, U0 + 128 * 1024]
        ringb = [Buf(), Buf()]
        rn = [0]

        rq = [[Buf() for _ in range(4)] for _ in range(2)]

        def prefetch_block(src_ap, nk, ncol):
            i = rn[0]
            rn[0] = (i + 1) % 2
            slot = V(RING_OFF[i], 128, [nk, ncol], BF16)
            srcv = src_ap.rearrange("(kd p) c -> p kd c", p=128)
            q = nk // 4
            for j in range(4):
                k.dma("pool", slot[:, j * q:(j + 1) * q, :], srcv[:, j * q:(j + 1) * q, :], writes=[rq[i][j]])
            return i, slot

        def stream_block(src_ap, nk, ncol, streams, extra_reads, first_start=True, pre=None):
            i, slot = pre if pre is not None else prefetch_block(src_ap, nk, ncol)
            q = nk // 4
            wb = [PSB[st_[0]] for st_ in streams]
            k.begin("pe", reads=extra_reads, writes=wb)
            for j in range(4):
                k.begin("pe", reads=[rq[i][j]])
                for kd in range(j * q, (j + 1) * q):
                    for (_, out_ap, lf, rf) in streams:
                        ins = nc.tensor.matmul(out_ap, lhsT=lf(slot, kd), rhs=rf(slot, kd), start=(kd == 0 and first_start), stop=(kd == nk - 1))
                if j < 3:
                    k.end("pe", ins, reads=[rq[i][j]])
                else:
                    k.end("pe", ins, reads=[rq[i][j]] + list(extra_reads), writes=wb)

        TOP0 = U0 + 144 * 1024
        CRt = V(U0, 33, [4096]); b_cr = Buf()
        SCT = V(TOP0, 128, [32, 33], BF16); b_sct = Buf()
        MODT = [V(TOP0 + 2176, 33, [256]), V(TOP0 + 3200, 33, [256])]; b_modt = [Buf(), Buf()]
        BROW = [V(TOP0 + 4224, 1, [256]), V(TOP0 + 5248, 1, [256])]; b_brow = [Buf(), Buf()]
        dv(lambda e: e.memset(CRt, 0.0), writes=[b_cr])
        k.dma("sp", CRt[0:16, :], cs, writes=[b_cr])
        k.dma("sp", CRt[32:33, :], cp, writes=[b_cr])
        k.op("act", lambda e: e.activation(out=CRt, in_=CRt, func=AF.Silu), reads=[b_cr], writes=[b_cr])
        for q in range(4):
            pi = nextps()
            k.begin("pe", reads=[b_cr, b_const], writes=[PSB[pi]])
            for j in range(8):
                kd = q * 8 + j
                ins = nc.tensor.matmul(PS[pi][:, j * 33:(j + 1) * 33], lhsT=CRt[0:33, kd * 128:(kd + 1) * 128],
                                       rhs=IDF[0:33, 0:33], start=True, stop=True)
            k.end("pe", ins, reads=[b_cr, b_const], writes=[PSB[pi]])
            k.op("act", lambda e, pi=pi, q=q: e.activation(
                out=SCT[:, q * 8:(q + 1) * 8, :], in_=PS[pi][:, 0:264].rearrange("p (a b) -> p a b", b=33), func=AF.Copy),
                 reads=[PSB[pi]], writes=[b_sct], par=True)
        k.barrier()

        def ada_block(blk):
            c0 = blk * 256
            bi = blk % 2
            k.dma("sp", BROW[bi], b_ada[c0:c0 + 256].rearrange("(o c) -> o c", o=1), writes=[b_brow[bi]])
            pi = nextps()
            k.op("pe", lambda e: e.matmul(PS[pi][0:33, 0:256], lhsT=ONES[0:1, 0:33], rhs=BROW[bi][0:1, :], start=True, stop=False),
                 reads=[b_const, b_brow[bi]], writes=[PSB[pi]])
            stream_block(w_ada[:, c0:c0 + 256], 32, 256,
                         [(pi, PS[pi][0:33, 0:256], lambda slot, kd: SCT[:, kd, :], lambda slot, kd: slot[:, kd, :])], [b_sct],
                         first_start=False)
            k.op("act", lambda e: e.activation(out=MODT[bi], in_=PS[pi][0:33, 0:256], func=AF.Copy),
                 reads=[PSB[pi]], writes=[b_modt[bi]])
            if blk >= 32:
                gc = c0 - 8192
                k.dma("sp", GROWD[:, gc:gc + 256], MODT[bi], reads=[b_modt[bi]])
                return
            p2 = nextps()
            k.begin("pe", reads=[b_modt[bi], b_const], writes=[PSB[p2]])
            for h in range(2):
                ins = nc.tensor.matmul(PS[p2][:, h * 33:(h + 1) * 33], lhsT=MODT[bi][0:33, h * 128:(h + 1) * 128],
                                       rhs=IDF[0:33, 0:33], start=True, stop=True)
            k.end("pe", ins, reads=[b_modt[bi], b_const], writes=[PSB[p2]])
            for h in range(2):
                kd = ((c0 % 4096) // 128) + h
                if blk < 16:
                    k.op("act", lambda e: e.activation(out=SHS[:, kd, :], in_=PS[p2][:, h * 33:h * 33 + 16], func=AF.Copy),
                         reads=[PSB[p2]], writes=[b_mod], par=True)
                    k.op("act", lambda e: e.activation(out=SHP[:, kd:kd + 1], in_=PS[p2][:, h * 33 + 32:h * 33 + 33], func=AF.Copy),
                         reads=[PSB[p2]], writes=[b_mod], par=True)
                else:
                    k.op("act", lambda e: e.activation(out=GSS[:, kd, :], in_=PS[p2][:, h * 33:h * 33 + 16], func=AF.Identity,
                                                       scale=NG[:, kd:kd + 1], bias=NG[:, kd:kd + 1]),
                         reads=[PSB[p2], b_const], writes=[b_mod], par=True)
                    k.op("act", lambda e: e.activation(out=GSP[:, kd:kd + 1], in_=PS[p2][:, h * 33 + 32:h * 33 + 33],
                                                       func=AF.Identity, scale=NG[:, kd:kd + 1], bias=NG[:, kd:kd + 1]),
                         reads=[PSB[p2], b_const], writes=[b_mod], par=True)

        PB = Buf()
        o = [U0]

        def alloc(parts, shape, dt=F32):
            n = 1
            for s in shape:
                n *= s
            sz = (2 if dt == BF16 else 4) * n
            v = V(o[0], parts, shape, dt)
            o[0] += (sz + 31) // 32 * 32
            return v

        def pd(fn):
            return dv(fn, reads=[PB, b_const], writes=[PB])

        def tt(out, a, b, op):
            pd(lambda e: e.tensor_tensor(out=out, in0=a, in1=b, op=op))

        def ts(out, a, s1, s2, op0, op1=None):
            if op1 is None:
                pd(lambda e: e.tensor_scalar(out=out, in0=a, scalar1=s1, scalar2=None, op0=op0))
            else:
                pd(lambda e: e.tensor_scalar(out=out, in0=a, scalar1=s1, scalar2=s2, op0=op0, op1=op1))

        LRAW = alloc(128, [64]); LIRAW = alloc(128, [64])
        LR = alloc(64, [128]); LI = alloc(64, [128]); LDT = alloc(64, [128])
        ZR = alloc(64, [128]); ZI = alloc(64, [128]); ACR = alloc(64, [128]); ACI = alloc(64, [128])
        T1 = alloc(64, [128]); T2 = alloc(64, [128]); T3 = alloc(64, [128]); T4 = alloc(64, [128])
        WR_ = alloc(64, [128]); WI_ = alloc(64, [128]); CFR = alloc(64, [128]); CFI = alloc(64, [128])
        ER = alloc(64, [128, 16]); EI = alloc(64, [128, 16]); ERZ = alloc(64, [128, 8]); EIZ = alloc(64, [128, 8])
        BBR = alloc(64, [128, 16]); BBI = alloc(64, [128, 16])
        CR = alloc(64, [128, 16]); CI = alloc(64, [128, 16])
        o_batch = o[0]
        BR = alloc(64, [128, 16]); BI = alloc(64, [128, 16]); TB = alloc(64, [128, 16])
        CRW = alloc(128, [16, 64]); CIW = alloc(128, [16, 64])
        k.dma("sp", LRAW, lam_re, writes=[PB]); k.dma("sp", LIRAW, lam_im, writes=[PB], append=True)
        k.dma("sp", LDT, log_dt.partition_broadcast(64), writes=[PB], append=True)
        k.dma("sp", BR, b_re.rearrange("g p h -> p g h"), writes=[PB], append=True)
        k.dma("sp", BI, b_im.rearrange("g p h -> p g h"), writes=[PB], append=True)
        k.dma("sp", CRW, c_re.rearrange("(gb g8) h p -> (g8 h) gb p", g8=8), writes=[PB], append=True)
        k.dma("sp", CIW, c_im.rearrange("(gb g8) h p -> (g8 h) gb p", g8=8), writes=[PB], append=True)
        for src, dst in ((LRAW, LR), (LIRAW, LI)):
            pi = nextps()
            k.op("pe", lambda e, pi=pi, src=src: e.matmul(PS[pi][0:64, 0:128], lhsT=src, rhs=IDF, start=True, stop=True),
                 reads=[PB, b_const], writes=[PSB[pi]])
            dv(lambda e, pi=pi, dst=dst: e.tensor_copy(out=dst, in_=PS[pi][0:64, 0:128]), reads=[PSB[pi], PB], writes=[PB])
        for src, dst in ((CRW, CR), (CIW, CI)):
            for q in range(4):
                pi = nextps()
                k.begin("pe", reads=[PB, b_const], writes=[PSB[pi]])
                for j in range(4):
                    gb = q * 4 + j
                    ins = nc.tensor.matmul(PS[pi][0:64, j * 128:(j + 1) * 128], lhsT=src[:, gb, :], rhs=IDF,
                                           start=True, stop=True)
                k.end("pe", ins, reads=[PB, b_const], writes=[PSB[pi]])
                dv(lambda e, pi=pi, dst=dst, q=q: e.tensor_copy(
                    out=dst[:, q * 32:(q + 1) * 32, :], in_=PS[pi][0:64, :].rearrange("p (g h) -> p g h", h=16)),
                   reads=[PSB[pi], PB], writes=[PB])
        k.op("act", lambda e: e.activation(out=LDT, in_=LDT, func=AF.Exp), reads=[PB], writes=[PB])
        for blk_ in range(12):
            ada_block(blk_)
        tt(ZR, LR, LDT, ALU.mult); tt(ZI, LI, LDT, ALU.mult)
        ts(ZR, ZR, 1.0 / 16, None, ALU.mult); ts(ZI, ZI, 1.0 / 16, None, ALU.mult)

        def cmul(outr, outi, ar, ai, br, bi):
            tt(T1, ar, br, ALU.mult); tt(T2, ai, bi, ALU.mult); tt(T3, ar, bi, ALU.mult); tt(T4, ai, br, ALU.mult)
            tt(outr, T1, T2, ALU.subtract); tt(outi, T3, T4, ALU.add)

        NTERM = 16
        pd(lambda e: e.memset(ACR, 1.0)); pd(lambda e: e.memset(ACI, 0.0))
        for n in range(NTERM, 1, -1):
            cmul(WR_, WI_, ZR, ZI, ACR, ACI)
            ts(ACR, WR_, 1.0 / n, 1.0, ALU.mult, ALU.add)
            ts(ACI, WI_, 1.0 / n, None, ALU.mult)
        cmul(WR_, WI_, ZR, ZI, ACR, ACI)
        for _ in range(4):
            cmul(ACR, ACI, WR_, WI_, WR_, WI_)
            pd(lambda e: e.scalar_tensor_tensor(out=WR_, in0=WR_, scalar=2.0, in1=ACR, op0=ALU.mult, op1=ALU.add))
            pd(lambda e: e.scalar_tensor_tensor(out=WI_, in0=WI_, scalar=2.0, in1=ACI, op0=ALU.mult, op1=ALU.add))
        tt(T1, LR, LR, ALU.mult); tt(T2, LI, LI, ALU.mult); tt(T1, T1, T2, ALU.add)
        pd(lambda e: e.reciprocal(out=ACR, in_=T1))
        tt(T1, WR_, LR, ALU.mult); tt(T2, WI_, LI, ALU.mult); tt(T1, T1, T2, ALU.add); tt(CFR, T1, ACR, ALU.mult)
        tt(T1, WI_, LR, ALU.mult); tt(T2, WR_, LI, ALU.mult); tt(T1, T1, T2, ALU.subtract); tt(CFI, T1, ACR, ALU.mult)
        ts(ACR, WR_, 1.0, None, ALU.add)
        pd(lambda e: e.tensor_copy(out=ACI, in_=WI_))
        pd(lambda e: e.memset(ER[:, :, 7], 1.0)); pd(lambda e: e.memset(EI[:, :, 7], 0.0))
        pd(lambda e: e.tensor_copy(out=ER[:, :, 8], in_=ACR)); pd(lambda e: e.tensor_copy(out=EI[:, :, 8], in_=ACI))
        for kk in range(9, 16):
            cmul(ER[:, :, kk], EI[:, :, kk], ER[:, :, kk - 1], EI[:, :, kk - 1], ACR, ACI)
        for kq in range(1, 8):
            tt(T1, ER[:, :, 7 + kq], ER[:, :, 7 + kq], ALU.mult); tt(T2, EI[:, :, 7 + kq], EI[:, :, 7 + kq], ALU.mult)
            tt(T1, T1, T2, ALU.add)
            pd(lambda e: e.reciprocal(out=T3, in_=T1))
            tt(ER[:, :, 7 - kq], ER[:, :, 7 + kq], T3, ALU.mult)
            pd(lambda e, kq=kq: e.scalar_tensor_tensor(out=EI[:, :, 7 - kq], in0=EI[:, :, 7 + kq], scalar=-1.0, in1=T3,
                                                       op0=ALU.mult, op1=ALU.mult))
        for s in range(8):
            pd(lambda e, s=s: e.tensor_copy(out=ERZ[:, :, s], in_=ER[:, :, 14 - s]))
            pd(lambda e, s=s: e.tensor_copy(out=EIZ[:, :, s], in_=EI[:, :, 14 - s]))
        for dst, kk in ((A8R, 15), (A4R, 11)):
            pd(lambda e, dst=dst, kk=kk: e.tensor_copy(out=dst, in_=ER[:, :, kk]))
        for dst, kk in ((A8I, 15), (A4I, 11)):
            pd(lambda e, dst=dst, kk=kk: e.tensor_copy(out=dst, in_=EI[:, :, kk]))
        bc16 = lambda a: a.unsqueeze(2).broadcast_to([64, 128, 16])
        tt(BBR, BR, bc16(CFR), ALU.mult); tt(TB, BI, bc16(CFI), ALU.mult); tt(BBR, BBR, TB, ALU.subtract)
        tt(BBI, BI, bc16(CFR), ALU.mult); tt(TB, BR, bc16(CFI), ALU.mult); tt(BBI, BBI, TB, ALU.add)
        GB = 8
        o[0] = o_batch
        ZRb = alloc(64, [GB, 8, 16]); ZIb = alloc(64, [GB, 8, 16]); TZb = alloc(64, [GB, 8, 16])
        CLRb = alloc(64, [GB, 8, 16]); CLIb = alloc(64, [GB, 8, 16])
        WORb = [alloc(64, [GB, 128], BF16) for _ in range(2)]; WOIb = [alloc(64, [GB, 128], BF16) for _ in range(2)]
        WSRb = [alloc(128, [GB, 64], BF16) for _ in range(2)]; WSIb = [alloc(128, [GB, 64], BF16) for _ in range(2)]
        TPb = [alloc(128, [GB, 128], BF16) for _ in range(2)]
        TF1 = alloc(128, [4, 128]); TF2 = alloc(128, [4, 128]); TWb = alloc(64, [GB, 8, 16])
        TZ2b = TZb; PBW = Buf(); PBC = PB
        DTb = [alloc(128, [GB * 16]) for _ in range(2)]; b_dt = [Buf(), Buf()]
        assert o[0] <= U0 + 112 * 1024, o[0]
        b_out = [Buf(), Buf()]
        ada_next = [12]
        for bt in range(128 // GB):
            g0 = bt * GB
            par = bt % 2
            gs_ = slice(g0, g0 + GB)
            k.dma("sp", DTb[par], d_skip[g0 * 16:(g0 + GB) * 16].partition_broadcast(128), writes=[b_dt[par]])
            bs = lambda a: a[:, gs_, :].unsqueeze(2).broadcast_to([64, GB, 8, 16])
            be = lambda a, lo: a[:, gs_, lo:lo + 8].unsqueeze(3).broadcast_to([64, GB, 8, 16])
            tt(ZRb, bs(BBR), be(ERZ, 0), ALU.mult); tt(TZb, bs(BBI), be(EIZ, 0), ALU.mult); tt(ZRb, ZRb, TZb, ALU.subtract)
            tt(ZIb, bs(BBI), be(ERZ, 0), ALU.mult); tt(TZb, bs(BBR), be(EIZ, 0), ALU.mult); tt(ZIb, ZIb, TZb, ALU.add)
            tt(CLRb, bs(CR), be(ER, 0), ALU.mult); tt(TZb, bs(CI), be(EI, 0), ALU.mult); tt(CLRb, CLRb, TZb, ALU.subtract)
            tt(CLIb, bs(CR), be(EI, 0), ALU.mult); tt(TZb, bs(CI), be(ER, 0), ALU.mult); tt(CLIb, CLIb, TZb, ALU.add)
            ts(CLIb, CLIb, -1.0, None, ALU.mult)
            wor4 = WORb[par].rearrange("p g (t h) -> p g t h", h=16); woi4 = WOIb[par].rearrange("p g (t h) -> p g t h", h=16)
            def wout_ops():
                w2 = dict(reads=[b_const, PBW, PBC], writes=[PBW])
                dv(lambda e: e.tensor_tensor(out=TWb, in0=bs(CR), in1=be(ER, 8), op=ALU.mult), **w2)
                dv(lambda e: e.tensor_tensor(out=TZ2b, in0=bs(CI), in1=be(EI, 8), op=ALU.mult), **w2)
                dv(lambda e: e.tensor_tensor(out=wor4, in0=TWb, in1=TZ2b, op=ALU.subtract), reads=[PBW], writes=[PBW, b_out[par]])
                dv(lambda e: e.tensor_tensor(out=TWb, in0=bs(CR), in1=be(EI, 8), op=ALU.mult), **w2)
                dv(lambda e: e.tensor_tensor(out=TZ2b, in0=bs(CI), in1=be(ER, 8), op=ALU.mult), **w2)
                dv(lambda e: e.tensor_tensor(out=TWb, in0=TWb, in1=TZ2b, op=ALU.add), reads=[PBW], writes=[PBW])
                dv(lambda e: e.tensor_scalar(out=woi4, in0=TWb, scalar1=-1.0, scalar2=None, op0=ALU.mult), reads=[PBW],
                   writes=[PBW, b_out[par]])

            for src, dstb in ((ZRb, WSRb[par]), (ZIb, WSIb[par])):
                for q in range(GB // 8):
                    pi = nextps()
                    k.begin("pe", reads=[PB, b_const], writes=[PSB[pi]])
                    for j in range(8):
                        gi = q * 8 + j
                        ins = nc.tensor.matmul(PS[pi][:, j * 64:(j + 1) * 64],
                                               lhsT=src[:, gi, :, :].rearrange("p s h -> p (s h)"),
                                               rhs=IDF[0:64, 0:64], start=True, stop=True)
                    k.end("pe", ins, reads=[PB, b_const], writes=[PSB[pi]])
                    dv(lambda e, pi=pi, dstb=dstb, q=q: e.tensor_copy(
                        out=dstb[:, q * 8:(q + 1) * 8, :], in_=PS[pi][:, :].rearrange("p (g c) -> p g c", c=64)),
                       reads=[PSB[pi]], writes=[b_out[par]])
            toep_pis = []
            for q in range(GB // 4):
                pi = nextps()
                toep_pis.append(pi)
                k.begin("pe", reads=[PB, b_const], writes=[PSB[pi]])
                for j in range(4):
                    gi = q * 4 + j
                    nc.tensor.matmul(PS[pi][:, j * 128:(j + 1) * 128], lhsT=ZRb[:, gi, :, :].rearrange("p s h -> p (s h)"),
                                     rhs=CLRb[:, gi, :, :].rearrange("p s h -> p (s h)"), start=True, stop=False)
                    ins = nc.tensor.matmul(PS[pi][:, j * 128:(j + 1) * 128], lhsT=ZIb[:, gi, :, :].rearrange("p s h -> p (s h)"),
                                           rhs=CLIb[:, gi, :, :].rearrange("p s h -> p (s h)"), start=False, stop=True)
                k.end("pe", ins, reads=[PB, b_const], writes=[PSB[pi]])
            wout_ops()
            for q in range(GB // 4):
                pi = toep_pis[q]
                gq = g0 + q * 4
                dv(lambda e, pi=pi: e.tensor_tensor(out=TF1, in0=PS[pi][:, :].rearrange("p (g c) -> p g c", c=128),
                                                    in1=MASK.unsqueeze(1).broadcast_to([128, 4, 128]), op=ALU.mult),
                   reads=[PSB[pi], b_const, PB], writes=[PB])
                dv(lambda e, gq=gq: e.tensor_tensor(
                    out=TF2[:, :, :].rearrange("p g (t h) -> p g t h", h=16),
                    in0=IDF.rearrange("p (t h) -> p t h", h=16).unsqueeze(1).broadcast_to([128, 4, 8, 16]),
                    in1=DTb[par][:, q * 64:(q + 1) * 64].rearrange("p (g h) -> p g h", h=16).unsqueeze(2).broadcast_to([128, 4, 8, 16]),
                    op=ALU.mult), reads=[PB, b_const, b_dt[par]], writes=[PB])
                dv(lambda e, q=q, par=par: e.tensor_tensor(out=TPb[par][:, q * 4:(q + 1) * 4, :], in0=TF1, in1=TF2, op=ALU.add),
                   reads=[PB], writes=[b_out[par]])
            for q in range(0):
                pi = nextps()
                k.begin("pe", reads=[PB, b_const], writes=[PSB[pi]])
                for j in range(4):
                    gi = q * 4 + j
                    nc.tensor.matmul(PS[pi][:, j * 128:(j + 1) * 128], lhsT=ZRb[:, gi, :, :].rearrange("p s h -> p (s h)"),
                                     rhs=CLRb[:, gi, :, :].rearrange("p s h -> p (s h)"), start=True, stop=False)
                    ins = nc.tensor.matmul(PS[pi][:, j * 128:(j + 1) * 128], lhsT=ZIb[:, gi, :, :].rearrange("p s h -> p (s h)"),
                                           rhs=CLIb[:, gi, :, :].rearrange("p s h -> p (s h)"), start=False, stop=True)
                k.end("pe", ins, reads=[PB, b_const], writes=[PSB[pi]])
                gq = g0 + q * 4
                dv(lambda e, pi=pi: e.tensor_tensor(out=TF1, in0=PS[pi][:, :].rearrange("p (g c) -> p g c", c=128),
                                                    in1=MASK.unsqueeze(1).broadcast_to([128, 4, 128]), op=ALU.mult),
                   reads=[PSB[pi], b_const, PB], writes=[PB])
                dv(lambda e, gq=gq: e.tensor_tensor(
                    out=TF2[:, :, :].rearrange("p g (t h) -> p g t h", h=16),
                    in0=IDF.rearrange("p (t h) -> p t h", h=16).unsqueeze(1).broadcast_to([128, 4, 8, 16]),
                    in1=DTb[par][:, q * 64:(q + 1) * 64].rearrange("p (g h) -> p g h", h=16).unsqueeze(2).broadcast_to([128, 4, 8, 16]),
                    op=ALU.mult), reads=[PB, b_const, b_dt[par]], writes=[PB])
                dv(lambda e, q=q, par=par: e.tensor_tensor(out=TPb[par][:, q * 4:(q + 1) * 4, :], in0=TF1, in1=TF2, op=ALU.add),
                   reads=[PB], writes=[b_out[par]])
            k.dma("sp", WSTR[:, gs_, :], WSRb[par], reads=[b_out[par]])
            k.dma("sp", WSTI[:, gs_, :], WSIb[par], reads=[b_out[par]])
            k.dma("sp", TOEP[:, gs_, :], TPb[par], reads=[b_out[par]])
            k.dma("sp", WOR[:, gs_, :], WORb[par], reads=[b_out[par]])
            k.dma("sp", WOI[:, gs_, :], WOIb[par], reads=[b_out[par]])
            for _ in range(3 if bt < 4 else 2):
                if ada_next[0] < 48:
                    ada_block(ada_next[0])
                    ada_next[0] += 1
        assert ada_next[0] == 48
        k.barrier()

        XN = V(U0, 128, [32, 512], BF16); b_xn = Buf()
        UA = V(U0 + 32768, 128, [16, 512], BF16); b_ua = Buf()
        R2 = U0 + 49152
        ZSA = V(R2, 128, [16, 512], BF16); VB = V(R2 + 16384, 128, [16, 512], BF16)
        MERGED = V(R2 + 32768, 128, [32, 512], BF16)
        b_zsa = [Buf() for _ in range(16)]; b_vb = [Buf() for _ in range(16)]; b_mg = [Buf() for _ in range(32)]
        TOP = U0 + 144 * 1024
        assert TOP + 12 * 1024 <= ARENA_BYTES

        def phase_norm(x_ap, ntok, sample, XN, b_xn, save_halo=False):
            XT = [V(R2, 128, [4096]), V(R2 + 16384, 128, [4096])]; b_xt = [Buf(), Buf()]
            XH = [V(R2 + 32768, 128, [4096], BF16), V(R2 + 40960, 128, [4096], BF16)]; b_xh = [Buf(), Buf()]
            TMPF = [V(TOP, 128, [1024]), V(TOP + 4096, 128, [1024])]; b_tf = [Buf(), Buf()]
            ntt = (ntok + 127) // 128
            tfn = [0]

            def stage_a(tt_):
                rows = min(128, ntok - tt_ * 128)
                s_ = tt_ % 2
                k.dma("sp", XT[s_][0:rows, :], x_ap[tt_ * 128:tt_ * 128 + rows, :], writes=[b_xt[s_]])
                k.op("act", lambda e: e.activation(out=XH[s_][0:rows, :], in_=XT[s_][0:rows, :], func=AF.Square,
                                                   accum_out=SS[0:rows, :]),
                     reads=[b_xt[s_]], writes=[b_xh[s_], b_small])
                k.op("act", lambda e: e.activation(out=RS[0:rows, :], in_=SS[0:rows, :], func=AF.Sqrt, bias=1e-6,
                                                   scale=1.0 / D), reads=[b_small], writes=[b_small])
                dv(lambda e: e.reciprocal(out=RS[0:rows, :], in_=RS[0:rows, :]), reads=[b_small], writes=[b_small])
                k.op("act", lambda e: e.activation(out=XH[s_][0:rows, :], in_=XT[s_][0:rows, :], func=AF.Copy,
                                                   scale=RS[0:rows, 0:1]),
                     reads=[b_xt[s_], b_small], writes=[b_xh[s_]])

            def stage_b(tt_):
                rows = min(128, ntok - tt_ * 128)
                s_ = tt_ % 2
                for q in range(4):
                    pi = nextps()
                    psb = PS[pi][:, :].bitcast(BF16)
                    k.begin("pe", reads=[b_xh[s_], b_const], writes=[PSB[pi]])
                    for j in range(8):
                        kd = q * 8 + j
                        ins = nc.tensor.transpose(psb[:, j * 128:j * 128 + rows], XH[s_][0:rows, kd * 128:(kd + 1) * 128],
                                                  IDB[0:rows, 0:rows])
                    k.end("pe", ins, reads=[b_xh[s_], b_const], writes=[PSB[pi]])
                    ti = tfn[0] % 2
                    tfn[0] += 1
                    pv = psb.rearrange("p (a b) -> p a b", b=128)[:, :, 0:rows]
                    tf = TMPF[ti].rearrange("p (a b) -> p a b", b=128)[:, :, 0:rows]
                    xo = XN[:, q * 8:(q + 1) * 8, tt_ * 128:tt_ * 128 + rows]
                    if not sample:
                        g_ = GSP[:, q * 8:(q + 1) * 8].unsqueeze(2).broadcast_to([128, 8, rows])
                        h_ = SHP[:, q * 8:(q + 1) * 8].unsqueeze(2).broadcast_to([128, 8, rows])
                        dv(lambda e: e.tensor_tensor(out=tf, in0=pv, in1=g_, op=ALU.mult),
                           reads=[PSB[pi], b_mod], writes=[b_tf[ti]])
                        dv(lambda e: e.tensor_tensor(out=xo, in0=tf, in1=h_, op=ALU.add),
                           reads=[b_tf[ti], b_mod], writes=[b_xn], par=True, e="pool")
                    else:
                        pv4 = pv.rearrange("p a (s t) -> p a s t", t=4)
                        tf4 = tf.rearrange("p a (s t) -> p a s t", t=4)
                        xo4 = xo.rearrange("p a (s t) -> p a s t", t=4)
                        g_ = GSS[:, q * 8:(q + 1) * 8, :].unsqueeze(3).broadcast_to([128, 8, 16, 4])
                        h_ = SHS[:, q * 8:(q + 1) * 8, :].unsqueeze(3).broadcast_to([128, 8, 16, 4])
                        dv(lambda e: e.tensor_tensor(out=tf4, in0=pv4, in1=g_, op=ALU.mult),
                           reads=[PSB[pi], b_mod], writes=[b_tf[ti]])
                        dv(lambda e: e.tensor_tensor(out=xo4, in0=tf4, in1=h_, op=ALU.add),
                           reads=[b_tf[ti], b_mod], writes=[b_xn], par=True, e="pool")

            stage_a(0)
            for tt_ in range(ntt):
                if tt_ + 1 < ntt:
                    stage_a(tt_ + 1)
                stage_b(tt_)
            if save_halo:
                dv(lambda e: e.tensor_copy(out=XNH, in_=XN[:, :, ntok - 2:ntok]), reads=[b_xn], writes=[b_prev])

        def evac_copy(i, out, in_, reads, writes):
            if i % 2 == 0:
                return k.op("act", lambda e: e.activation(out=out, in_=in_, func=AF.Copy), reads=reads, writes=writes, par=True)
            return dv(lambda e: e.tensor_copy(out=out, in_=in_), reads=reads, writes=writes, par=True)

        class Seg:
            pass

        SP_ = Seg(); SP_.N = 512; SP_.nseq = 1; SP_.L = 512; SP_.sample = False
        SP_.XN = XN; SP_.UA = UA; SP_.ZSA = ZSA; SP_.VB = VB; SP_.MG = MERGED
        SP_.b_xn = b_xn; SP_.b_ua = b_ua; SP_.b_zsa = b_zsa; SP_.b_vb = b_vb; SP_.b_mg = b_mg
        SS_ = Seg(); SS_.N = 64; SS_.nseq = 16; SS_.L = 4; SS_.sample = True
        SS_.XN = XN_S; SS_.UA = UA_S; SS_.ZSA = ZSA_S; SS_.VB = VB_S; SS_.MG = MERGED_S
        SS_.b_xn = Buf(); SS_.b_ua = Buf(); SS_.b_zsa = [Buf() for _ in range(16)]; SS_.b_vb = [Buf() for _ in range(16)]
        SS_.b_mg = [Buf() for _ in range(32)]

        def seg_streams(segs, rhs_of, lhs_of_m):
            streams = []
            outs = [dict(), dict()]
            ps_s = None
            for m in range(2):
                for sg in segs:
                    if not sg.sample:
                        pi = nextps()
                        o_ = PS[pi][:, 0:sg.N]
                    else:
                        pi = nextps()
                        o_ = PS[pi][:, 0:sg.N]
                    outs[m][sg] = (pi, o_)
                    streams.append((pi, o_, lhs_of_m(m), rhs_of(sg)))
            return streams, outs

        def ua_prefetch():
            return [prefetch_block(w_in[:, c * 256:(c + 1) * 256], 32, 256) for c in range(2)]

        def phase_ua(segs, pre=None):
            for ctp in range(8):
                streams, outs = seg_streams(segs, (lambda sg: (lambda slot, kd: sg.XN[:, kd, 0:sg.N])),
                                            (lambda m: (lambda slot, kd: slot[:, kd, m * 128:(m + 1) * 128])))
                stream_block(w_in[:, ctp * 256:(ctp + 1) * 256], 32, 256, streams, [sg.b_xn for sg in segs],
                             pre=(pre[ctp] if (pre is not None and ctp < len(pre)) else None))
                for m in range(2):
                    ct = ctp * 2 + m
                    for sg in segs:
                        pi, o_ = outs[m][sg]
                        evac_copy(ct, sg.UA[:, ct, 0:sg.N], o_, [PSB[pi]], [sg.b_ua])

        CM = V(R2, 64, [128, 128], BF16); b_cm = Buf()
        UT = V(R2 + 32768, 128, [128 * 32], BF16); b_ut = Buf()
        SRE = V(R2 + 40960, 64, [128 * 32]); SIM = V(R2 + 57344, 64, [128 * 32]); b_s = Buf()
        HHR = V(R2 + 73728, 64, [128 * 32], BF16); HHI = V(R2 + 81920, 64, [128 * 32], BF16); b_hh = Buf()
        MATO = R2 + 90112; b_ms = [Buf() for _ in range(3)]; b_my = [Buf() for _ in range(4)]
        SCT_ = [V(TOP + i * 512, 64, [128]) for i in range(6)]; b_sc = Buf(); b_scs = [Buf() for _ in range(6)]
        assert MATO + 6144 <= TOP
        H0R = V(R2 + 40960 + 8192, 64, [128 * 16]); H0I = V(R2 + 57344 + 8192, 64, [128 * 16])
        hpar = [0]

        def s5_subpass(t0, nj, sample, state_only, UA=UA, b_ua=b_ua):
            T = 4 if sample else 8
            slots = [4, 5, 6, 7] if sample else list(range(8))
            ns = len(slots)
            c0 = 64 if sample else 0
            UTv = UT[:, 0:128 * nj].rearrange("p (g j) -> p g j", j=nj)
            SREv = SRE[:, 0:128 * nj].rearrange("p (g j) -> p g j", j=nj)
            SIMv = SIM[:, 0:128 * nj].rearrange("p (g j) -> p g j", j=nj)
            HHRv = HHR[:, 0:128 * nj].rearrange("p (g j) -> p g j", j=nj)
            HHIv = HHI[:, 0:128 * nj].rearrange("p (g j) -> p g j", j=nj)
            CM4 = CM.rearrange("p g (s h) -> p g s h", h=16)
            CMF = CM.rearrange("p g c -> p (g c)")
            CMY = CMF.rearrange("p (t g h) -> p t g h", t=8, g=128, h=16)
            if sample:
                dv(lambda e: e.memset(CM[0:nj, :, 0:64], 0.0), writes=[b_cm])
            for gb in range(16):
                pi = nextps()
                psb = PS[pi][:, :].bitcast(BF16)
                uv = UA[:, gb, t0:t0 + T * nj].rearrange("p (j s) -> p s j", s=T)
                k.begin("pe", reads=[b_ua, b_const], writes=[PSB[pi]])
                for si in range(ns):
                    ins = nc.tensor.transpose(psb[0:nj, si * 128:(si + 1) * 128], uv[:, si if not sample else si, :], IDB)
                k.end("pe", ins, reads=[b_ua, b_const], writes=[PSB[pi]])
                evac_copy(gb, CM4[0:nj, gb * 8:(gb + 1) * 8, slots[0]:slots[0] + ns, :],
                          psb[0:nj, 0:ns * 128].rearrange("p (s g h) -> p g s h", s=ns, g=8, h=16), [PSB[pi]], [b_cm])
            gpb = 1024 // nj
            for q in range(128 // gpb):
                pi = nextps()
                psb = PS[pi][:, :].bitcast(BF16)
                k.begin("pe", reads=[b_cm, b_const], writes=[PSB[pi]])
                for j in range(gpb):
                    g = q * gpb + j
                    ins = nc.tensor.transpose(psb[:, j * nj:(j + 1) * nj], CM[0:nj, g, :], IDB[0:nj, 0:nj])
                k.end("pe", ins, reads=[b_cm, b_const], writes=[PSB[pi]])
                evac_copy(q, UTv[:, q * gpb:(q + 1) * gpb, :], psb[:, 0:gpb * nj].rearrange("p (g j) -> p g j", j=nj),
                          [PSB[pi]], [b_ut])
            for bt in range(16):
                g0 = bt * 8
                mi = bt % 3
                wsr = V(MATO + mi * 2048, 128, [8, 64], BF16); wsi = V(MATO + mi * 2048 + 1024, 128, [8, 64], BF16)
                k.dma("sp", wsr, WSTR[:, g0:g0 + 8, :], writes=[b_ms[mi]])
                k.dma("sp", wsi, WSTI[:, g0:g0 + 8, :], writes=[b_ms[mi]], append=True)
                p1 = nextps(); p2 = nextps()
                k.begin("pe", reads=[b_ms[mi], b_ut], writes=[PSB[p1], PSB[p2]])
                for j in range(8):
                    nc.tensor.matmul(PS[p1][0:64, j * nj:(j + 1) * nj], lhsT=wsr[:, j, :], rhs=UTv[:, g0 + j, :],
                                     start=True, stop=True)
                    ins = nc.tensor.matmul(PS[p2][0:64, j * nj:(j + 1) * nj], lhsT=wsi[:, j, :], rhs=UTv[:, g0 + j, :],
                                           start=True, stop=True)
                k.end("pe", ins, reads=[b_ms[mi], b_ut], writes=[PSB[p1], PSB[p2]])
                evac_copy(0, SREv[:, g0:g0 + 8, :], PS[p1][0:64, 0:8 * nj].rearrange("p (g j) -> p g j", j=nj), [PSB[p1]], [b_s])
                evac_copy(1, SIMv[:, g0:g0 + 8, :], PS[p2][0:64, 0:8 * nj].rearrange("p (g j) -> p g j", j=nj), [PSB[p2]], [b_s])
            t1, t2, t3, t4, t5, t6 = SCT_
            if not sample:
                for j in range(nj):
                    hr, hi = HST[hpar[0]]
                    nr, ni = HST[1 - hpar[0]]
                    if not state_only:
                        k.op("act", lambda e: e.activation(out=HHRv[:, :, j], in_=hr, func=AF.Copy), reads=[b_hst], writes=[b_hh], par=True)
                        k.op("act", lambda e: e.activation(out=HHIv[:, :, j], in_=hi, func=AF.Copy), reads=[b_hst], writes=[b_hh], par=True)
                    bt_ = b_scs
                    rd = [b_hst, b_const]
                    dv(lambda e: e.tensor_tensor(out=t1, in0=A8R, in1=hr, op=ALU.mult), reads=rd, writes=[bt_[0]])
                    dv(lambda e: e.tensor_tensor(out=t2, in0=A8I, in1=hi, op=ALU.mult), reads=rd, writes=[bt_[1]])
                    dv(lambda e: e.tensor_tensor(out=t4, in0=A8R, in1=hi, op=ALU.mult), reads=rd, writes=[bt_[3]])
                    dv(lambda e: e.tensor_tensor(out=t5, in0=A8I, in1=hr, op=ALU.mult), reads=rd, writes=[bt_[4]])
                    dv(lambda e: e.tensor_tensor(out=t3, in0=t1, in1=t2, op=ALU.subtract), reads=[bt_[0], bt_[1]], writes=[bt_[2]])
                    dv(lambda e: e.tensor_tensor(out=t6, in0=t4, in1=t5, op=ALU.add), reads=[bt_[3], bt_[4]], writes=[bt_[5]])
                    dv(lambda e: e.tensor_tensor(out=nr, in0=t3, in1=SREv[:, :, j], op=ALU.add),
                       reads=[bt_[2], b_s], writes=[b_hst], par=True)
                    dv(lambda e: e.tensor_tensor(out=ni, in0=t6, in1=SIMv[:, :, j], op=ALU.add),
                       reads=[bt_[5], b_s], writes=[b_hst], par=True)
                    hpar[0] = 1 - hpar[0]
            else:
                H0Rv = H0R.rearrange("p (g j) -> p g j", j=16); H0Iv = H0I.rearrange("p (g j) -> p g j", j=16)
                dv(lambda e: e.tensor_copy(out=HHRv, in_=H0Rv), reads=[b_hst], writes=[b_hh])
                dv(lambda e: e.tensor_copy(out=HHIv, in_=H0Iv), reads=[b_hst], writes=[b_hh])
                a4r = A4R.unsqueeze(2).broadcast_to([64, 128, 16]); a4i = A4I.unsqueeze(2).broadcast_to([64, 128, 16])
                rw = dict(reads=[b_hst, b_s, b_const], writes=[b_s])
                TAv = V(R2, 64, [128, 16])
                dv(lambda e: e.tensor_tensor(out=TAv, in0=H0Rv, in1=a4r, op=ALU.mult), reads=[b_hst, b_const], writes=[b_cm])
                dv(lambda e: e.tensor_tensor(out=SREv, in0=SREv, in1=TAv, op=ALU.add), reads=[b_cm, b_s], writes=[b_s])
                dv(lambda e: e.tensor_tensor(out=TAv, in0=H0Iv, in1=a4i, op=ALU.mult), reads=[b_hst, b_const, b_s], writes=[b_cm])
                dv(lambda e: e.tensor_tensor(out=SREv, in0=SREv, in1=TAv, op=ALU.subtract), reads=[b_cm, b_s], writes=[b_s])
                dv(lambda e: e.tensor_tensor(out=TAv, in0=H0Iv, in1=a4r, op=ALU.mult), reads=[b_hst, b_const, b_s], writes=[b_cm])
                dv(lambda e: e.tensor_tensor(out=SIMv, in0=SIMv, in1=TAv, op=ALU.add), reads=[b_cm, b_s], writes=[b_s])
                dv(lambda e: e.tensor_tensor(out=TAv, in0=H0Rv, in1=a4i, op=ALU.mult), reads=[b_hst, b_const, b_s], writes=[b_cm])
                dv(lambda e: e.tensor_tensor(out=SIMv, in0=SIMv, in1=TAv, op=ALU.add), reads=[b_cm, b_s], writes=[b_s])
            if state_only:
                return
            for bt in range(16):
                g0 = bt * 8
                if sample:
                    mi = 0
                    yo = MATO
                    ybufs = [b_my[0]] + b_ms
                else:
                    mi = bt % 4
                    yo = R2 + 40960 + mi * 6144
                    ybufs = [b_my[mi], b_s]
                tp = V(yo, 128, [8, 128], BF16)
                wr = V(yo + 2048, 64, [8, 128], BF16); wi = V(yo + 4096, 64, [8, 128], BF16)
                k.dma("sp", tp, TOEP[:, g0:g0 + 8, :], writes=ybufs)
                k.dma("sp", wr, WOR[:, g0:g0 + 8, :], writes=ybufs, append=True)
                k.dma("sp", wi, WOI[:, g0:g0 + 8, :], writes=ybufs, append=True)
                for hlf in range(2):
                    pi = nextps()
                    k.begin("pe", reads=ybufs + [b_ut, b_hh], writes=[PSB[pi]])
                    for j in range(4):
                        gi = hlf * 4 + j
                        g = g0 + gi
                        nc.tensor.matmul(PS[pi][0:nj, j * 128:(j + 1) * 128], lhsT=UTv[:, g, :], rhs=tp[:, gi, :],
                                         start=True, stop=False)
                        nc.tensor.matmul(PS[pi][0:nj, j * 128 + c0:(j + 1) * 128], lhsT=HHRv[:, g, :], rhs=wr[:, gi, 0:128 - c0],
                                         start=False, stop=False)
                        ins = nc.tensor.matmul(PS[pi][0:nj, j * 128 + c0:(j + 1) * 128], lhsT=HHIv[:, g, :],
                                               rhs=wi[:, gi, 0:128 - c0], start=False, stop=True)
                    k.end("pe", ins, reads=ybufs + [b_ut, b_hh], writes=[PSB[pi]])
                    gq = g0 + hlf * 4
                    k.op("act", lambda e: e.activation(out=CMY[0:nj, :, gq:gq + 4, :],
                                                       in_=PS[pi][0:nj, :].rearrange("p (g t h) -> p t g h", g=4, t=8, h=16),
                                                       func=AF.Gelu),
                         reads=[PSB[pi]], writes=[b_cm], par=True)
            for gb in range(16):
                pi = nextps()
                psb = PS[pi][:, :].bitcast(BF16)
                k.begin("pe", reads=[b_cm, b_const], writes=[PSB[pi]])
                for ti in range(ns):
                    t = slots[ti]
                    ins = nc.tensor.transpose(psb[:, ti * nj:(ti + 1) * nj],
                                              CMF[0:nj, t * 2048 + gb * 128:t * 2048 + (gb + 1) * 128], IDB[0:nj, 0:nj])
                k.end("pe", ins, reads=[b_cm, b_const], writes=[PSB[pi]])
                yv = UA[:, gb, t0:t0 + T * nj].rearrange("p (j s) -> p s j", s=T)
                evac_copy(gb, yv, psb[:, 0:ns * nj].rearrange("p (s j) -> p s j", j=nj), [PSB[pi]], [b_ua])

        A0 = R2 + 32768
        SP_.HBT = [V(A0 + i * 2048, 128, [512]) for i in range(2)]
        SP_.FULL = [V(A0 + 4096 + i * 2304, 128, [1, 514]) for i in range(2)]
        SP_.CVO = [V(A0 + 8704 + i * 2048, 128, [512]) for i in range(2)]
        SP_.SZ = [V(A0 + 12800 + i * 2048, 128, [512]) for i in range(2)]
        SS_.HBT = [V(STMP + i * 256, 128, [64]) for i in range(2)]
        SS_.FULL = [V(STMP + 512 + i * 384, 128, [16, 6]) for i in range(2)]
        SS_.CVO = [V(STMP + 1280 + i * 256, 128, [64]) for i in range(2)]
        SS_.SZ = [V(STMP + 1792 + i * 256, 128, [64]) for i in range(2)]
        HBH = [V(A0 + 16896 + i * 8, 128, [2]) for i in range(2)]
        PRH = [V(A0 + 16912 + i * 8, 128, [2]) for i in range(2)]
        b_hbh = [Buf(), Buf()]
        for sg in (SP_, SS_):
            sg.b_hbt = [Buf(), Buf()]; sg.b_full = [Buf(), Buf()]; sg.b_cvo = [Buf(), Buf()]; sg.b_sz = [Buf(), Buf()]

        def phase_A(segs, halo):
            def mm_block(kind, ctp, want_halo):
                streams, outs = seg_streams(segs, (lambda sg: (lambda slot, kd: sg.XN[:, kd, 0:sg.N])),
                                            (lambda m: (lambda slot, kd: slot[:, kd, m * 128:(m + 1) * 128])))
                phs = [None, None]
                if want_halo:
                    for m in range(2):
                        ph = nextps()
                        phs[m] = ph
                        streams.append((ph, PS[ph][:, 0:2], (lambda slot, kd, m=m: slot[:, kd, m * 128:(m + 1) * 128]),
                                        (lambda slot, kd: XNH[:, kd, :])))
                stream_block(w_in[:, kind * 2048 + ctp * 256: kind * 2048 + (ctp + 1) * 256], 32, 256, streams,
                             [sg.b_xn for sg in segs] + [b_prev])
                return outs, phs

            for ctp in range(8):
                cts = [ctp * 2, ctp * 2 + 1]
                outs, _ = mm_block(1, ctp, False)
                for m in range(2):
                    for sg in segs:
                        pi, o_ = outs[m][sg]
                        k.op("act", lambda e: e.activation(out=sg.ZSA[:, cts[m], 0:sg.N], in_=o_, func=AF.Silu),
                             reads=[PSB[pi]], writes=[sg.b_zsa[cts[m]]])
                outs, phs = mm_block(2, ctp, halo)
                for m in range(2):
                    for sg in segs:
                        pi, o_ = outs[m][sg]
                        k.op("act", lambda e: e.activation(out=sg.HBT[m][:, 0:sg.N], in_=o_, func=AF.Copy),
                             reads=[PSB[pi]], writes=[sg.b_hbt[m]])
                    if halo:
                        ph = phs[m]
                        k.op("act", lambda e: e.activation(out=HBH[m], in_=PS[ph][:, 0:2], func=AF.Copy),
                             reads=[PSB[ph]], writes=[b_hbh[m]])
                outs, phs = mm_block(3, ctp, halo)
                for m in range(2):
                    ct = cts[m]
                    if halo:
                        ph = phs[m]
                        dv(lambda e: e.tensor_tensor(out=PRH[m], in0=HBH[m], in1=PS[ph][:, 0:2], op=ALU.mult),
                           reads=[PSB[ph], b_hbh[m]], writes=[b_hbh[m]])
                        dv(lambda e: e.tensor_scalar(out=PREVP[:, ct, :], in0=PRH[m], scalar1=FLAG[:, 0:1], scalar2=None,
                                                     op0=ALU.mult), reads=[b_hbh[m], b_const], writes=[b_prev])
                    for sg in segs:
                        pi, o_ = outs[m][sg]
                        fl = sg.FULL[m]
                        L = sg.L
                        N = sg.N
                        if sg.sample:
                            dv(lambda e: e.tensor_copy(out=fl[:, :, 0:2], in_=PREVS[:, ct, :, :]), reads=[b_prev],
                               writes=[sg.b_full[m]])
                        else:
                            dv(lambda e: e.tensor_copy(out=fl[:, :, 0:2], in_=PREVP[:, ct, :].unsqueeze(1)), reads=[b_prev],
                               writes=[sg.b_full[m]])
                        dv(lambda e: e.tensor_tensor(out=fl[:, :, 2:2 + L], in0=sg.HBT[m][:, 0:N].rearrange("p (s t) -> p s t", t=L),
                                                     in1=o_.rearrange("p (s t) -> p s t", t=L), op=ALU.mult),
                           reads=[PSB[pi], sg.b_hbt[m], sg.b_full[m]], writes=[sg.b_full[m]])
                        if sg.sample:
                            dv(lambda e: e.tensor_copy(out=CONVS[:, ct, :, :], in_=fl[:, :, L:L + 2]), reads=[sg.b_full[m]],
                               writes=[b_prev])
                        else:
                            dv(lambda e: e.tensor_copy(out=PREVP[:, ct, :].unsqueeze(1), in_=fl[:, :, L:L + 2]),
                               reads=[sg.b_full[m]], writes=[b_prev])
                        cv = sg.CVO[m][:, 0:N].rearrange("p (s t) -> p s t", t=L)
                        dv(lambda e: e.tensor_scalar(out=cv, in0=fl[:, :, 0:L], scalar1=CW[:, 0, ct:ct + 1], scalar2=None,
                                                     op0=ALU.mult), reads=[sg.b_full[m], b_const], writes=[sg.b_cvo[m]])
                        dv(lambda e: e.scalar_tensor_tensor(out=cv, in0=fl[:, :, 1:1 + L], scalar=CW[:, 1, ct:ct + 1], in1=cv,
                                                            op0=ALU.mult, op1=ALU.add),
                           reads=[sg.b_full[m], b_const, sg.b_cvo[m]], writes=[sg.b_cvo[m]])
                        dv(lambda e: e.scalar_tensor_tensor(out=cv, in0=fl[:, :, 2:2 + L], scalar=CW[:, 2, ct:ct + 1], in1=cv,
                                                            op0=ALU.mult, op1=ALU.add),
                           reads=[sg.b_full[m], b_const, sg.b_cvo[m]], writes=[sg.b_cvo[m]])
                outs, _ = mm_block(4, ctp, False)
                for m in range(2):
                    for sg in segs:
                        pi, o_ = outs[m][sg]
                        dv(lambda e: e.tensor_tensor(out=sg.CVO[m][:, 0:sg.N], in0=sg.CVO[m][:, 0:sg.N], in1=o_, op=ALU.mult),
                           reads=[PSB[pi], sg.b_cvo[m]], writes=[sg.b_cvo[m]])
                outs, _ = mm_block(5, ctp, False)
                for m in range(2):
                    for sg in segs:
                        pi, o_ = outs[m][sg]
                        k.op("act", lambda e: e.activation(out=sg.SZ[m][:, 0:sg.N], in_=o_, func=AF.Silu),
                             reads=[PSB[pi]], writes=[sg.b_sz[m]])
                        dv(lambda e: e.tensor_tensor(out=sg.VB[:, cts[m], 0:sg.N], in0=sg.CVO[m][:, 0:sg.N],
                                                     in1=sg.SZ[m][:, 0:sg.N], op=ALU.mult),
                           reads=[sg.b_cvo[m], sg.b_sz[m]], writes=[sg.b_vb[cts[m]]])
            for ctp in range(8):
                streams, outs = seg_streams(segs, (lambda sg: (lambda slot, kd: sg.UA[:, kd, 0:sg.N])),
                                            (lambda m: (lambda slot, kd: slot[:, kd, m * 128:(m + 1) * 128])))
                stream_block(w_glu[:, ctp * 256:(ctp + 1) * 256], 16, 256, streams, [sg.b_ua for sg in segs])
                for m in range(2):
                    ct = ctp * 2 + m
                    for sg in segs:
                        pi, o_ = outs[m][sg]
                        N = sg.N
                        k.op("act", lambda e: e.activation(out=sg.SZ[m][:, 0:N], in_=o_, func=AF.Sigmoid,
                                                           bias=BGLU[:, ct:ct + 1]), reads=[PSB[pi], b_const], writes=[sg.b_sz[m]])
                        dv(lambda e: e.tensor_tensor(out=sg.SZ[m][:, 0:N], in0=sg.SZ[m][:, 0:N], in1=sg.UA[:, ct, 0:N], op=ALU.mult),
                           reads=[sg.b_sz[m], sg.b_ua], writes=[sg.b_sz[m]])
                        dv(lambda e: e.tensor_tensor(out=sg.ZSA[:, ct, 0:N], in0=sg.SZ[m][:, 0:N], in1=sg.ZSA[:, ct, 0:N],
                                                     op=ALU.mult), reads=[sg.b_sz[m], sg.b_zsa[ct]], writes=[sg.b_zsa[ct]])

        SP_.BT = [V(TOP + i * 2048, 128, [512]) for i in range(4)]
        SS_.BT = [V(STMP + i * 256, 128, [64]) for i in range(4)]
        for sg in (SP_, SS_):
            sg.b_bt = [Buf() for _ in range(4)]

        def phase_B(segs):
            for dtp in range(16):
                def run_block(src_ap, nk, attr, battr):
                    def rhs_of(sg):
                        t_ = getattr(sg, attr)
                        return lambda slot, kd: t_[:, kd, 0:sg.N]
                    streams, outs = seg_streams(segs, rhs_of, (lambda m: (lambda slot, kd: slot[:, kd, m * 128:(m + 1) * 128])))
                    rb = []
                    for sg in segs:
                        bb = getattr(sg, battr)
                        rb += list(bb) if isinstance(bb, list) else [bb]
                    stream_block(src_ap, nk, 256, streams, rb)
                    return outs

                outs = run_block(w_in[:, 12288 + dtp * 256:12288 + (dtp + 1) * 256], 32, "XN", "b_xn")
                for m in range(2):
                    for sg in segs:
                        pi, o_ = outs[m][sg]
                        k.op("act", lambda e: e.activation(out=sg.BT[m][:, 0:sg.N], in_=o_, func=AF.Sigmoid),
                             reads=[PSB[pi]], writes=[sg.b_bt[m]])
                outs = run_block(w_pa[:, dtp * 256:(dtp + 1) * 256], 16, "ZSA", "b_zsa")
                for m in range(2):
                    for sg in segs:
                        pi, o_ = outs[m][sg]
                        dv(lambda e: e.tensor_tensor(out=sg.BT[m][:, 0:sg.N], in0=sg.BT[m][:, 0:sg.N], in1=o_, op=ALU.mult),
                           reads=[PSB[pi], sg.b_bt[m]], writes=[sg.b_bt[m]])
                outs = run_block(w_in[:, 16384 + dtp * 256:16384 + (dtp + 1) * 256], 32, "XN", "b_xn")
                for m in range(2):
                    for sg in segs:
                        pi, o_ = outs[m][sg]
                        k.op("act", lambda e: e.activation(out=sg.BT[2 + m][:, 0:sg.N], in_=o_, func=AF.Sigmoid),
                             reads=[PSB[pi]], writes=[sg.b_bt[2 + m]])
                outs = run_block(w_pb[:, dtp * 256:(dtp + 1) * 256], 16, "VB", "b_vb")
                for m in range(2):
                    dt_ = dtp * 2 + m
                    for sg in segs:
                        pi, o_ = outs[m][sg]
                        N = sg.N
                        dv(lambda e: e.tensor_tensor(out=sg.BT[2 + m][:, 0:N], in0=sg.BT[2 + m][:, 0:N], in1=o_, op=ALU.mult),
                           reads=[PSB[pi], sg.b_bt[2 + m]], writes=[sg.b_bt[2 + m]])
                        dv(lambda e: e.tensor_tensor(out=sg.MG[:, dt_, 0:N], in0=sg.BT[m][:, 0:N], in1=sg.BT[2 + m][:, 0:N],
                                                     op=ALU.add), reads=[sg.b_bt[m], sg.b_bt[2 + m]], writes=[sg.b_mg[dt_]])

        def phase_C(groups):
            tiles = []
            for (x_ap, y_ap, ntok, sg) in groups:
                for t_ in range((ntok + 127) // 128):
                    rows = min(128, ntok - t_ * 128)
                    tiles.append((x_ap[t_ * 128:t_ * 128 + rows, :], y_ap[t_ * 128:t_ * 128 + rows, :], rows, sg, t_ * 128))
            nt = len(tiles)
            assert nt <= 5
            Hh = [V(U0 + i * 16384, 128, [4096]) for i in range(nt)]; b_h = [Buf() for _ in range(nt)]
            GCB = [V(TOP + i * 1024, 128, [256]) for i in range(2)]; b_gcb = [Buf(), Buf()]
            GCS = [V(TOP + 6400 + i * 1024, 64, [256]) for i in range(2)]; b_gcs = [Buf(), Buf()]
            TMO = [V(TOP + 2048 + i * 1024, 128, [256]) for i in range(2)]; b_tmo = [Buf(), Buf()]
            REP = V(TOP + 4096, 16, [64]); b_rep = Buf()
            GR = [V(TOP + 4352 + i * 1024, 33, [256]) for i in range(2)]; b_gr = [Buf(), Buf()]
            has_p = any(not t[3].sample for t in tiles)
            has_s = any(t[3].sample for t in tiles)
            for i, (xr, yr, rows, sg, c0) in enumerate(tiles):
                k.dma("sp", Hh[i][0:rows, :], xr, writes=[b_h[i]])
            if has_s:
                dv(lambda e: e.tensor_copy(out=REP.rearrange("p (s t) -> p s t", t=4),
                                           in_=IDF[0:16, 0:16].unsqueeze(2).broadcast_to([16, 16, 4])),
                   reads=[b_const], writes=[b_rep])
            mg_bufs = []
            for sg in set(t[3] for t in tiles):
                mg_bufs += list(sg.b_mg)
            for cb in range(16):
                gi = cb % 2
                k.dma("sp", GR[gi], GROWD[:, cb * 256:(cb + 1) * 256], writes=[b_gr[gi]])
                if has_p:
                    pg = nextps()
                    k.op("pe", lambda e: e.matmul(PS[pg][:, 0:256], lhsT=ONES[32:33, :], rhs=GR[gi][32:33, :], start=True, stop=True),
                         reads=[b_const, b_gr[gi]], writes=[PSB[pg]])
                    k.op("act", lambda e: e.activation(out=GCB[gi], in_=PS[pg][:, 0:256], func=AF.Copy),
                         reads=[PSB[pg]], writes=[b_gcb[gi]])
                if has_s:
                    pg2 = nextps()
                    k.op("pe", lambda e: e.matmul(PS[pg2][0:64, 0:256], lhsT=REP, rhs=GR[gi][0:16, :], start=True, stop=True),
                         reads=[b_rep, b_gr[gi]], writes=[PSB[pg2]])
                    k.op("act", lambda e: e.activation(out=GCS[gi], in_=PS[pg2][0:64, 0:256], func=AF.Copy),
                         reads=[PSB[pg2]], writes=[b_gcs[gi]])
                pis = [nextps() for _ in range(nt)]
                streams = []
                for i, (xr, yr, rows, sg, c0) in enumerate(tiles):
                    if sg.sample:
                        streams.append((pis[i], PS[pis[i]][0:128, 0:256],
                                        (lambda slot, kd: MGS_FLAT[:, kd * 64:kd * 64 + 128]),
                                        (lambda slot, kd: slot[:, kd, :])))
                    else:
                        streams.append((pis[i], PS[pis[i]][0:rows, 0:256],
                                        (lambda slot, kd, sg=sg, c0=c0, rows=rows: sg.MG[:, kd, c0:c0 + rows]),
                                        (lambda slot, kd: slot[:, kd, :])))
                stream_block(w_o[:, cb * 256:(cb + 1) * 256], 32, 256, streams, mg_bufs)
                for i, (xr, yr, rows, sg, c0) in enumerate(tiles):
                    pi = pis[i]
                    ti = i % 2
                    gsrc = GCS[gi] if sg.sample else GCB[gi]
                    gbuf = b_gcs[gi] if sg.sample else b_gcb[gi]
                    dv(lambda e: e.tensor_tensor(out=TMO[ti][0:rows, :], in0=PS[pi][0:rows, 0:256], in1=gsrc[0:rows, :],
                                                 op=ALU.mult), reads=[PSB[pi], gbuf], writes=[b_tmo[ti]])
                    hv = Hh[i][0:rows, cb * 256:(cb + 1) * 256]
                    dv(lambda e: e.tensor_tensor(out=hv, in0=hv, in1=TMO[ti][0:rows, :], op=ALU.add),
                       reads=[b_tmo[ti], b_h[i]], writes=[b_h[i]])
            k.barrier()
            JUNK = V(U0 + 81920, 128, [4096], BF16); b_junk = Buf()
            FGB = V(U0 + 90112, 128, [4096]); b_fg = Buf()
            k.dma("sp", FGB, final_g.partition_broadcast(128), writes=[b_fg])
            for i, (xr, yr, rows, sg, c0) in enumerate(tiles):
                hv = Hh[i][0:rows, :]
                SSi = V(TOP + 9216 + i * 8, 128, [1]); RSi = V(TOP + 9220 + i * 8, 128, [1]); b_smi = Buf()
                k.op("act", lambda e: e.activation(out=JUNK[0:rows, :], in_=hv, func=AF.Square, accum_out=SSi[0:rows, :]),
                     reads=[b_h[i]], writes=[b_junk, b_smi])
                k.op("act", lambda e: e.activation(out=RSi[0:rows, :], in_=SSi[0:rows, :], func=AF.Sqrt, bias=1e-6, scale=1.0 / D),
                     reads=[b_smi], writes=[b_smi])
                dv(lambda e: e.reciprocal(out=RSi[0:rows, :], in_=RSi[0:rows, :]), reads=[b_smi], writes=[b_smi])
                dv(lambda e: e.scalar_tensor_tensor(out=hv, in0=hv, scalar=RSi[0:rows, 0:1], in1=FGB[0:rows, :], op0=ALU.mult,
                                                    op1=ALU.mult), reads=[b_h[i], b_smi, b_fg], writes=[b_h[i]])
                k.dma("sp", yr, hv, reads=[b_h[i]])

        for u in range(2):
            pre_ = ua_prefetch()
            phase_norm(xp[u * 512:(u + 1) * 512, :], 512, False, XN, b_xn, save_halo=(u == 1))
            phase_ua([SP_], pre=pre_)
            k.barrier()
            for sp_ in range(2):
                s5_subpass(sp_ * 256, 32, False, True)
            k.barrier()
        hr, hi = HST[hpar[0]]
        dv(lambda e: e.tensor_scalar(out=hr, in0=hr, scalar1=FLAG[0:64, 0:1], scalar2=None, op0=ALU.mult), reads=[b_hst, b_const],
           writes=[b_hst])
        dv(lambda e: e.tensor_scalar(out=hi, in0=hi, scalar1=FLAG[0:64, 0:1], scalar2=None, op0=ALU.mult), reads=[b_hst, b_const],
           writes=[b_hst])
        for u in range(2):
            merged_unit = (u == 1)
            segs = [SP_, SS_] if merged_unit else [SP_]
            xa = xm[u * 512:(u + 1) * 512, :]
            pre_ = None if merged_unit else ua_prefetch()
            phase_norm(xa, 512, False, XN, b_xn)
            if merged_unit:
                k.barrier()
                pre_ = ua_prefetch()
                phase_norm(xs, 64, True, XN_S, SS_.b_xn)
            phase_ua(segs, pre=pre_)
            k.barrier()
            for sp_ in range(2):
                s5_subpass(sp_ * 256, 32, False, False)
            k.barrier()
            if merged_unit:
                SRAW = V(R2, 128, [16, 128]); b_sraw = Buf()
                k.dma("sp", SRAW, sssm.rearrange("s g p c -> g s (p c)"), writes=[b_sraw])
                H0Rv = H0R.rearrange("p (g j) -> p g j", j=16); H0Iv = H0I.rearrange("p (g j) -> p g j", j=16)
                for c_, dst in ((0, H0Rv), (1, H0Iv)):
                    for q in range(4):
                        pi = nextps()
                        k.begin("pe", reads=[b_sraw, b_const], writes=[PSB[pi]])
                        for j in range(4):
                            sq = q * 4 + j
                            ins = nc.tensor.matmul(PS[pi][0:64, j * 128:(j + 1) * 128],
                                                   lhsT=SRAW[:, sq, :].rearrange("p (a c) -> p a c", c=2)[:, :, c_], rhs=IDF,
                                                   start=True, stop=True)
                        k.end("pe", ins, reads=[b_sraw, b_const], writes=[PSB[pi]])
                        dv(lambda e: e.tensor_copy(out=dst[:, :, q * 4:(q + 1) * 4].rearrange("p g s -> p s g"),
                                                   in_=PS[pi][0:64, :].rearrange("p (s g) -> p s g", g=128)),
                           reads=[PSB[pi]], writes=[b_hst])
                CVR = V(R2 + 8192, 32, [2048]); b_cvr = Buf()
                k.dma("sp", CVR, sconv.rearrange("s r c -> (s r) c"), writes=[b_cvr])
                for q in range(4):
                    pi = nextps()
                    k.begin("pe", reads=[b_cvr, b_const], writes=[PSB[pi]])
                    for j in range(4):
                        ct = q * 4 + j
                        ins = nc.tensor.matmul(PS[pi][:, j * 32:(j + 1) * 32], lhsT=CVR[:, ct * 128:(ct + 1) * 128],
                                               rhs=IDF[0:32, 0:32], start=True, stop=True)
                    k.end("pe", ins, reads=[b_cvr, b_const], writes=[PSB[pi]])
                    dv(lambda e: e.tensor_copy(out=PREVS[:, q * 4:(q + 1) * 4, :, :].rearrange("p c s r -> p c (s r)"),
                                               in_=PS[pi][:, 0:128].rearrange("p (c x) -> p c x", x=32)),
                       reads=[PSB[pi]], writes=[b_prev])
                k.barrier()
                s5_subpass(0, 16, True, False, UA=UA_S, b_ua=SS_.b_ua)
                k.barrier()
                SREv = SRE[:, 0:128 * 16].rearrange("p (g j) -> p g j", j=16)
                SIMv = SIM[:, 0:128 * 16].rearrange("p (g j) -> p g j", j=16)
                NSS = V(R2, 128, [16, 64, 2])
                b_nss = Buf()
                for c_, src in ((0, SREv), (1, SIMv)):
                    for q in range(2):
                        pi = nextps()
                        k.begin("pe", reads=[b_s, b_const], writes=[PSB[pi]])
                        for j in range(8):
                            sq = q * 8 + j
                            ins = nc.tensor.matmul(PS[pi][:, j * 64:(j + 1) * 64], lhsT=src[:, :, sq], rhs=IDF[0:64, 0:64],
                                                   start=True, stop=True)
                        k.end("pe", ins, reads=[b_s, b_const], writes=[PSB[pi]])
                        dv(lambda e: e.tensor_copy(out=NSS[:, q * 8:(q + 1) * 8, :, c_],
                                                   in_=PS[pi][:, :].rearrange("p (s x) -> p s x", x=64)),
                           reads=[PSB[pi]], writes=[b_nss])
                k.dma("sp", nss.rearrange("s g p c -> g s (p c)"), NSS.rearrange("p s a c -> p s (a c)"), reads=[b_nss])
                k.barrier()
            phase_A(segs, halo=(u == 0))
            k.barrier()
            if merged_unit:
                NCS = V(TOP, 32, [2048]); b_ncs = Buf()
                for q in range(4):
                    pi = nextps()
                    k.begin("pe", reads=[b_prev, b_const], writes=[PSB[pi]])
                    for j in range(4):
                        ct = q * 4 + j
                        ins = nc.tensor.matmul(PS[pi][0:32, j * 128:(j + 1) * 128],
                                               lhsT=CONVS[:, ct, :, :].rearrange("p s r -> p (s r)"), rhs=IDF, start=True, stop=True)
                    k.end("pe", ins, reads=[b_prev, b_const], writes=[PSB[pi]])
                    dv(lambda e: e.tensor_copy(out=NCS[:, q * 512:(q + 1) * 512], in_=PS[pi][0:32, :]), reads=[PSB[pi]],
                       writes=[b_ncs])
                k.dma("sp", ncs.rearrange("s r c -> (s r) c"), NCS, reads=[b_ncs])
                k.barrier()
            phase_B(segs)
            k.barrier()
            if merged_unit:
                phase_C([(xa, yp[u * 512:(u + 1) * 512, :], 512, SP_), (xs, ys, 64, SS_)])
            else:
                phase_C([(xa, yp[u * 512:(u + 1) * 512, :], 512, SP_)])
            k.barrier()
        hr, hi = HST[hpar[0]]
        NSO = V(U0, 128, [64, 2]); b_nso = Buf()
        for c_, src in ((0, hr), (1, hi)):
            pi = nextps()
            k.op("pe", lambda e: e.matmul(PS[pi][:, 0:64], lhsT=src, rhs=IDF[0:64, 0:64], start=True, stop=True),
                 reads=[b_hst, b_const], writes=[PSB[pi]])
            dv(lambda e: e.tensor_copy(out=NSO[:, :, c_], in_=PS[pi][:, 0:64]), reads=[PSB[pi]], writes=[b_nso])
        k.dma("sp", nsp, NSO, reads=[b_nso])
        NCO = V(U0 + 1024, 2, [2048]); b_nco = Buf()
        for q in range(4):
            pi = nextps()
            k.begin("pe", reads=[b_prev, b_const], writes=[PSB[pi]])
            for j in range(4):
                ct = q * 4 + j
                ins = nc.tensor.matmul(PS[pi][0:2, j * 128:(j + 1) * 128], lhsT=PREVP[:, ct, :], rhs=IDF, start=True, stop=True)
            k.end("pe", ins, reads=[b_prev, b_const], writes=[PSB[pi]])
            dv(lambda e: e.tensor_copy(out=NCO[:, q * 512:(q + 1) * 512], in_=PS[pi][0:2, :]), reads=[PSB[pi]], writes=[b_nco])
        k.dma("sp", ncp, NCO, reads=[b_nco])
        k.barrier()
    return nc


_NC = None


def kernel(**inp):
    global _NC
    f = lambda a: np.ascontiguousarray(np.asarray(a, dtype=np.float32))
    x_prompt = f(inp["x_prompt"]); x_sample = f(inp["x_sample"])
    state_ssm = f(inp["state_ssm"]); state_conv = f(inp["state_conv"])
    c_prompt = f(inp["c_prompt"]); c_sample = f(inp["c_sample"])
    if _NC is None:
        _NC = build_nc()
    nc = _NC
    ident = np.eye(128, dtype=np.float32)
    mask = np.zeros((128, 128), np.float32)
    for s in range(8):
        for t in range(s, 8):
            mask[s * 16:(s + 1) * 16, t * 16:(t + 1) * 16] = 1.0
    shared = {
        "ident": ident, "mask": mask,
        "norm_g": f(inp["norm_g"])[0], "w_ada": f(inp["w_ada"])[0], "b_ada": f(inp["b_ada"])[0], "w_in": f(inp["w_in"])[0],
        "lam_re": f(inp["lam_re"])[0], "lam_im": f(inp["lam_im"])[0], "log_dt": f(inp["log_dt"])[0],
        "b_re": f(inp["b_re"])[0], "b_im": f(inp["b_im"])[0], "c_re": f(inp["c_re"])[0], "c_im": f(inp["c_im"])[0],
        "d_skip": f(inp["d_skip"])[0], "w_glu": f(inp["w_glu"])[0], "b_glu": f(inp["b_glu"])[0], "w_pa": f(inp["w_pa"])[0],
        "conv_w": f(inp["conv_w"])[0], "w_pb": f(inp["w_pb"])[0], "w_o": f(inp["w_o"])[0], "final_g": f(inp["final_g"]),
    }
    in_maps = []
    for c in range(8):
        b, half = c // 2, c % 2
        m = dict(shared)
        m["xm"] = np.ascontiguousarray(x_prompt[b, half * 1024:(half + 1) * 1024])
        m["xp"] = np.ascontiguousarray(x_prompt[b, 0:1024])
        m["xs"] = np.ascontiguousarray(x_sample[16 * c:16 * c + 16].reshape(64, D))
        m["cp"] = np.ascontiguousarray(c_prompt[b:b + 1])
        m["cs"] = np.ascontiguousarray(c_sample[16 * c:16 * c + 16])
        m["sssm"] = np.ascontiguousarray(state_ssm[0, 16 * c:16 * c + 16])
        m["sconv"] = np.ascontiguousarray(state_conv[0, 16 * c:16 * c + 16])
        m["flag"] = np.full((128, 1), float(half), np.float32)
        in_maps.append(m)
    res = run_bass_kernel_spmd(nc, in_maps, core_ids=list(range(8)))
    r = res.results
    y_prompt = np.empty((4, 2048, D), np.float32)
    y_sample = np.empty((128, 4, D), np.float32)
    nsp = np.empty((1, 4, 128, 64, 2), np.float32)
    ncp = np.empty((1, 4, 2, 2048), np.float32)
    nss = np.empty((1, 128, 128, 64, 2), np.float32)
    ncs = np.empty((1, 128, 2, 2048), np.float32)
    for c in range(8):
        b, half = c // 2, c % 2
        y_prompt[b, half * 1024:(half + 1) * 1024] = r[c]["yp"]
        y_sample[16 * c:16 * c + 16] = r[c]["ys"].reshape(16, 4, D)
        nss[0, 16 * c:16 * c + 16] = r[c]["nss"]
        ncs[0, 16 * c:16 * c + 16] = r[c]["ncs"]
        if half == 1:
            nsp[0, b] = r[c]["nsp"]
            ncp[0, b] = r[c]["ncp"]
    return (y_prompt, y_sample, nsp, ncp, nss, ncs)
```

```python
import contextlib
import math
import numpy as np
import concourse.bass as bass
import concourse.mybir as mybir
from concourse.bass_utils import run_bass_kernel_spmd

F32 = mybir.dt.float32
BF16 = mybir.dt.bfloat16
ALU = mybir.AluOpType
AF = mybir.ActivationFunctionType

D = 4096
NDS = 48
ARENA_BYTES = 188 * 1024


class Buf:
    def __init__(self):
        self.w = []
        self.r = []


class K:
    def __init__(self, nc, es):
        self.nc = nc
        self.E = {"pe": nc.tensor, "act": nc.scalar, "dve": nc.vector, "pool": nc.gpsimd, "sp": nc.sync}
        self.sem = {e: es.enter_context(nc.semaphore("sem_" + e)) for e in self.E}
        self.cnt = {e: 0 for e in self.E}
        self.waited = {e: {} for e in self.E}
        self.dsem = [es.enter_context(nc.semaphore("dsem%d" % i)) for i in range(NDS)]
        self.dcount = [0] * NDS
        self.dtok = [None] * NDS
        self.dnext = 0
        self.dnext2 = {"sp": 0, "pool": 0}

    def wait(self, e, tok):
        if tok is None:
            return
        if tok[0] == "e":
            if e == "pe" and tok[1] == "pe":
                return
            key = tok[1]
            if self.waited[e].get(key, 0) >= tok[2]:
                return
            self.E[e].wait_ge(self.sem[tok[1]], tok[2])
            self.waited[e][key] = tok[2]
        else:
            key = ("d", tok[1])
            if self.waited[e].get(key, 0) >= tok[2]:
                return
            self.E[e].wait_ge(self.dsem[tok[1]], tok[2])
            self.waited[e][key] = tok[2]

    def begin(self, e, reads=(), writes=(), par=False):
        for b in reads:
            for t in b.w:
                self.wait(e, t)
        for b in writes:
            if not par:
                for t in b.w:
                    self.wait(e, t)
            for t in b.r:
                self.wait(e, t)

    def _upd(self, e, tok, reads, writes, append=False, par=False):
        for b in writes:
            if par:
                b.w = [t for t in b.w if not (t[0] == "e" and tok[0] == "e" and t[1] == tok[1])] + [tok]
            elif append:
                b.w = b.w + [tok]
                b.r = []
            else:
                b.w = [tok]
                b.r = []
        for b in reads:
            if b in writes:
                continue
            if tok[0] == "e":
                b.r = [t for t in b.r if not (t[0] == "e" and t[1] == e)]
            b.r.append(tok)

    def end(self, e, ins, reads=(), writes=(), par=False):
        ins.then_inc(self.sem[e], 1)
        self.cnt[e] += 1
        tok = ("e", e, self.cnt[e])
        self._upd(e, tok, reads, writes, par=par)
        return tok

    def op(self, e, fn, reads=(), writes=(), par=False):
        self.begin(e, reads, writes, par=par)
        ins = fn(self.E[e])
        return self.end(e, ins, reads, writes, par=par)

    def dma(self, e, out, in_, reads=(), writes=(), append=False, **kw):
        if not append:
            self.begin(e, reads, writes)
        half = NDS // 2
        base = 0 if e == "sp" else half
        i = base + self.dnext2[e]
        self.dnext2[e] = (self.dnext2[e] + 1) % half
        self.wait(e, self.dtok[i])
        ins = self.E[e].dma_start(out=out, in_=in_, **kw)
        self.dcount[i] += 16
        ins.then_inc(self.dsem[i], 16)
        tok = ("d", i, self.dcount[i])
        self.dtok[i] = tok
        self._upd(e, tok, reads, writes, append)
        return tok

    def barrier(self):
        for e in self.E:
            for e2 in self.E:
                if e2 != e and self.cnt[e2] > 0:
                    self.wait(e, ("e", e2, self.cnt[e2]))
            if e == "pe":
                pass
            for t in self.dtok:
                self.wait(e, t)


def build_nc():
    nc = bass.Bass("TRN2", target_bir_lowering=False)
    di = lambda n, s, dt=F32: nc.dram_tensor(n, s, dt, kind="ExternalInput").ap()
    do = lambda n, s: nc.dram_tensor(n, s, F32, kind="ExternalOutput").ap()
    dint = lambda n, s, dt: nc.dram_tensor(n, s, dt, kind="Internal").ap()
    xm = di("xm", [1024, D]); xp = di("xp", [1024, D]); xs = di("xs", [64, D])
    cp = di("cp", [1, D]); cs = di("cs", [16, D])
    sssm = di("sssm", [16, 128, 64, 2]); sconv = di("sconv", [16, 2, 2048]); flag = di("flag", [128, 1])
    ident = di("ident", [128, 128]); maskd = di("mask", [128, 128])
    norm_g = di("norm_g", [D]); w_ada = di("w_ada", [D, 3 * D]); b_ada = di("b_ada", [3 * D])
    w_in = di("w_in", [D, 20480])
    lam_re = di("lam_re", [128, 64]); lam_im = di("lam_im", [128, 64]); log_dt = di("log_dt", [128])
    b_re = di("b_re", [128, 64, 16]); b_im = di("b_im", [128, 64, 16])
    c_re = di("c_re", [128, 16, 64]); c_im = di("c_im", [128, 16, 64])
    d_skip = di("d_skip", [2048]); w_glu = di("w_glu", [2048, 2048]); b_glu = di("b_glu", [2048])
    w_pa = di("w_pa", [2048, D]); conv_w = di("conv_w", [3, 2048]); w_pb = di("w_pb", [2048, D])
    w_o = di("w_o", [D, D]); final_g = di("final_g", [D])
    yp = do("yp", [1024, D]); ys = do("ys", [64, D]); nsp = do("nsp", [128, 64, 2]); ncp = do("ncp", [2, 2048])
    nss = do("nss", [16, 128, 64, 2]); ncs = do("ncs", [16, 2, 2048])
    WSTR = dint("WSTR", [128, 128, 64], BF16); WSTI = dint("WSTI", [128, 128, 64], BF16)
    TOEP = dint("TOEP", [128, 128, 128], BF16)
    WOR = dint("WOR", [64, 128, 128], BF16); WOI = dint("WOI", [64, 128, 128], BF16)
    GROWD = dint("GROWD", [33, 4096], F32)

    es = contextlib.ExitStack()
    with es:
        ARENA = es.enter_context(nc.sbuf_tensor("arena", [128, ARENA_BYTES // 2], BF16))
        PS = [es.enter_context(nc.psum_tensor("ps%d" % i, [128, 512], F32)) for i in range(8)]
        PSB = [Buf() for _ in range(8)]
        k = K(nc, es)
        psn = [0]

        def nextps():
            i = psn[0]
            psn[0] = (i + 1) % 8
            return i

        def V(off, parts, shape, dt=F32):
            n = 1
            for s in shape:
                n *= s
            sz = 2 if dt == BF16 else 4
            assert off % 4 == 0 and off + n * sz <= ARENA_BYTES, (off, shape)
            a = ARENA[0:parts, off // 2: off // 2 + n * sz // 2]
            if dt != BF16:
                a = a.bitcast(dt)
            if len(shape) > 1:
                names = ["d%d" % i for i in range(len(shape))]
                kw = {names[i]: shape[i] for i in range(1, len(shape))}
                a = a.rearrange("p (" + " ".join(names) + ") -> p " + " ".join(names), **kw)
            return a

        IDB = V(0, 128, [128], BF16); IDF = V(256, 128, [128]); MASK = V(768, 128, [128])
        A8R = V(1280, 64, [128]); A8I = V(1792, 64, [128]); A4R = V(2304, 64, [128]); A4I = V(2816, 64, [128])
        HST = [[V(3328, 64, [128]), V(3840, 64, [128])], [V(4352, 64, [128]), V(4864, 64, [128])]]
        GSP = V(5376, 128, [32]); SHP = V(5504, 128, [32]); NG = V(5632, 128, [32])
        BGLU = V(5760, 128, [16]); CW = V(5824, 128, [3, 16]); PREVP = V(6016, 128, [16, 2])
        FLAG = V(6144, 128, [1]); SS = V(6148, 128, [1]); RS = V(6152, 128, [1]); ONES = V(6160, 128, [128])
        GSS = V(6672, 128, [32, 16]); SHS = V(8720, 128, [32, 16]); PREVS = V(10768, 128, [16, 16, 2])
        CONVS = V(12816, 128, [16, 16, 2]); XNH = V(14864, 128, [32, 2], BF16)
        XN_S = V(15040, 128, [32, 64], BF16); UA_S = V(19136, 128, [16, 64], BF16); ZSA_S = V(21184, 128, [16, 64], BF16)
        VB_S = V(23232, 128, [16, 64], BF16); MERGED_S = V(25280, 128, [32, 64], BF16)
        MGS_FLAT = V(25280, 128, [33 * 64], BF16)
        STMP = 29504
        U0 = 32768
        b_const = Buf(); b_mod = Buf(); b_hst = Buf(); b_prev = Buf(); b_small = Buf(); b_grow = Buf()

        def dv(fn, reads=(), writes=(), e="dve", par=False):
            return k.op(e, fn, reads, writes, par=par)

        k.dma("sp", IDF, ident, writes=[b_const])
        k.dma("sp", MASK, maskd, writes=[b_const], append=True)
        k.dma("sp", FLAG, flag, writes=[b_const], append=True)
        k.dma("sp", NG, norm_g.rearrange("(k p) -> p k", p=128), writes=[b_const], append=True, allow_slow_non_contiguous=True)
        k.dma("sp", BGLU, b_glu.rearrange("(k p) -> p k", p=128), writes=[b_const], append=True, allow_slow_non_contiguous=True)
        k.dma("sp", CW, conv_w.rearrange("r (k p) -> p r k", p=128), writes=[b_const], append=True, allow_slow_non_contiguous=True)
        dv(lambda e: e.tensor_copy(out=IDB, in_=IDF), reads=[b_const], writes=[b_const])
        dv(lambda e: e.memset(ONES, 1.0), writes=[b_const])
        for st in (HST[0][0], HST[0][1], HST[1][0], HST[1][1]):
            dv(lambda e, st=st: e.memset(st, 0.0), writes=[b_hst])
        dv(lambda e: e.memset(PREVP, 0.0), writes=[b_prev])
        dv(lambda e: e.memset(MGS_FLAT[:, 32 * 64:33 * 64], 0.0), writes=[b_const])

        RING_OFF = [U0 + 112 * 1024, U0 + 128 * 1024]
        ringb = [Buf(), Buf()]
        rn = [0]

        rq = [[Buf() for _ in range(4)] for _ in range(2)]

        def stream_block(src_ap, nk, ncol, streams, extra_reads, first_start=True):
            i = rn[0]
            rn[0] = (i + 1) % 2
            slot = V(RING_OFF[i], 128, [nk, ncol], BF16)
            srcv = src_ap.rearrange("(kd p) c -> p kd c", p=128)
            q = nk // 4
            for j in range(4):
                k.dma("pool", slot[:, j * q:(j + 1) * q, :], srcv[:, j * q:(j + 1) * q, :], writes=[rq[i][j]])
            wb = [PSB[st_[0]] for st_ in streams]
            k.begin("pe", reads=extra_reads, writes=wb)
            for j in range(4):
                k.begin("pe", reads=[rq[i][j]])
                for kd in range(j * q, (j + 1) * q):
                    for (_, out_ap, lf, rf) in streams:
                        ins = nc.tensor.matmul(out_ap, lhsT=lf(slot, kd), rhs=rf(slot, kd), start=(kd == 0 and first_start), stop=(kd == nk - 1))
                if j < 3:
                    k.end("pe", ins, reads=[rq[i][j]])
                else:
                    k.end("pe", ins, reads=[rq[i][j]] + list(extra_reads), writes=wb)

        TOP0 = U0 + 144 * 1024
        CRt = V(U0, 33, [4096]); b_cr = Buf()
        SCT = V(TOP0, 128, [32, 33], BF16); b_sct = Buf()
        MODT = [V(TOP0 + 2176, 33, [256]), V(TOP0 + 3200, 33, [256])]; b_modt = [Buf(), Buf()]
        BROW = [V(TOP0 + 4224, 1, [256]), V(TOP0 + 5248, 1, [256])]; b_brow = [Buf(), Buf()]
        dv(lambda e: e.memset(CRt, 0.0), writes=[b_cr])
        k.dma("sp", CRt[0:16, :], cs, writes=[b_cr])
        k.dma("sp", CRt[32:33, :], cp, writes=[b_cr])
        k.op("act", lambda e: e.activation(out=CRt, in_=CRt, func=AF.Silu), reads=[b_cr], writes=[b_cr])
        for q in range(4):
            pi = nextps()
            k.begin("pe", reads=[b_cr, b_const], writes=[PSB[pi]])
            for j in range(8):
                kd = q * 8 + j
                ins = nc.tensor.matmul(PS[pi][:, j * 33:(j + 1) * 33], lhsT=CRt[0:33, kd * 128:(kd + 1) * 128],
                                       rhs=IDF[0:33, 0:33], start=True, stop=True)
            k.end("pe", ins, reads=[b_cr, b_const], writes=[PSB[pi]])
            k.op("act", lambda e, pi=pi, q=q: e.activation(
                out=SCT[:, q * 8:(q + 1) * 8, :], in_=PS[pi][:, 0:264].rearrange("p (a b) -> p a b", b=33), func=AF.Copy),
                 reads=[PSB[pi]], writes=[b_sct], par=True)
        k.barrier()

        def ada_block(blk):
            c0 = blk * 256
            bi = blk % 2
            k.dma("sp", BROW[bi], b_ada[c0:c0 + 256].rearrange("(o c) -> o c", o=1), writes=[b_brow[bi]])
            pi = nextps()
            k.op("pe", lambda e: e.matmul(PS[pi][0:33, 0:256], lhsT=ONES[0:1, 0:33], rhs=BROW[bi][0:1, :], start=True, stop=False),
                 reads=[b_const, b_brow[bi]], writes=[PSB[pi]])
            stream_block(w_ada[:, c0:c0 + 256], 32, 256,
                         [(pi, PS[pi][0:33, 0:256], lambda slot, kd: SCT[:, kd, :], lambda slot, kd: slot[:, kd, :])], [b_sct],
                         first_start=False)
            k.op("act", lambda e: e.activation(out=MODT[bi], in_=PS[pi][0:33, 0:256], func=AF.Copy),
                 reads=[PSB[pi]], writes=[b_modt[bi]])
            if blk >= 32:
                gc = c0 - 8192
                k.dma("sp", GROWD[:, gc:gc + 256], MODT[bi], reads=[b_modt[bi]])
                return
            p2 = nextps()
            k.begin("pe", reads=[b_modt[bi], b_const], writes=[PSB[p2]])
            for h in range(2):
                ins = nc.tensor.matmul(PS[p2][:, h * 33:(h + 1) * 33], lhsT=MODT[bi][0:33, h * 128:(h + 1) * 128],
                                       rhs=IDF[0:33, 0:33], start=True, stop=True)
            k.end("pe", ins, reads=[b_modt[bi], b_const], writes=[PSB[p2]])
            for h in range(2):
                kd = ((c0 % 4096) // 128) + h
                if blk < 16:
                    k.op("act", lambda e: e.activation(out=SHS[:, kd, :], in_=PS[p2][:, h * 33:h * 33 + 16], func=AF.Copy),
                         reads=[PSB[p2]], writes=[b_mod], par=True)
                    k.op("act", lambda e: e.activation(out=SHP[:, kd:kd + 1], in_=PS[p2][:, h * 33 + 32:h * 33 + 33], func=AF.Copy),
                         reads=[PSB[p2]], writes=[b_mod], par=True)
                else:
                    k.op("act", lambda e: e.activation(out=GSS[:, kd, :], in_=PS[p2][:, h * 33:h * 33 + 16], func=AF.Identity,
                                                       scale=NG[:, kd:kd + 1], bias=NG[:, kd:kd + 1]),
                         reads=[PSB[p2], b_const], writes=[b_mod], par=True)
                    k.op("act", lambda e: e.activation(out=GSP[:, kd:kd + 1], in_=PS[p2][:, h * 33 + 32:h * 33 + 33],
                                                       func=AF.Identity, scale=NG[:, kd:kd + 1], bias=NG[:, kd:kd + 1]),
                         reads=[PSB[p2], b_const], writes=[b_mod], par=True)

        PB = Buf()
        o = [U0]

        def alloc(parts, shape, dt=F32):
            n = 1
            for s in shape:
                n *= s
            sz = (2 if dt == BF16 else 4) * n
            v = V(o[0], parts, shape, dt)
            o[0] += (sz + 31) // 32 * 32
            return v

        def pd(fn):
            return dv(fn, reads=[PB, b_const], writes=[PB])

        def tt(out, a, b, op):
            pd(lambda e: e.tensor_tensor(out=out, in0=a, in1=b, op=op))

        def ts(out, a, s1, s2, op0, op1=None):
            if op1 is None:
                pd(lambda e: e.tensor_scalar(out=out, in0=a, scalar1=s1, scalar2=None, op0=op0))
            else:
                pd(lambda e: e.tensor_scalar(out=out, in0=a, scalar1=s1, scalar2=s2, op0=op0, op1=op1))

        LRAW = alloc(128, [64]); LIRAW = alloc(128, [64])
        LR = alloc(64, [128]); LI = alloc(64, [128]); LDT = alloc(64, [128])
        ZR = alloc(64, [128]); ZI = alloc(64, [128]); ACR = alloc(64, [128]); ACI = alloc(64, [128])
        T1 = alloc(64, [128]); T2 = alloc(64, [128]); T3 = alloc(64, [128]); T4 = alloc(64, [128])
        WR_ = alloc(64, [128]); WI_ = alloc(64, [128]); CFR = alloc(64, [128]); CFI = alloc(64, [128])
        ER = alloc(64, [128, 16]); EI = alloc(64, [128, 16]); ERZ = alloc(64, [128, 8]); EIZ = alloc(64, [128, 8])
        BBR = alloc(64, [128, 16]); BBI = alloc(64, [128, 16])
        CR = alloc(64, [128, 16]); CI = alloc(64, [128, 16])
        o_batch = o[0]
        BR = alloc(64, [128, 16]); BI = alloc(64, [128, 16]); TB = alloc(64, [128, 16])
        CRW = alloc(128, [16, 64]); CIW = alloc(128, [16, 64])
        k.dma("sp", LRAW, lam_re, writes=[PB]); k.dma("sp", LIRAW, lam_im, writes=[PB], append=True)
        k.dma("sp", LDT, log_dt.partition_broadcast(64), writes=[PB], append=True)
        k.dma("sp", BR, b_re.rearrange("g p h -> p g h"), writes=[PB], append=True)
        k.dma("sp", BI, b_im.rearrange("g p h -> p g h"), writes=[PB], append=True)
        k.dma("sp", CRW, c_re.rearrange("(gb g8) h p -> (g8 h) gb p", g8=8), writes=[PB], append=True)
        k.dma("sp", CIW, c_im.rearrange("(gb g8) h p -> (g8 h) gb p", g8=8), writes=[PB], append=True)
        for src, dst in ((LRAW, LR), (LIRAW, LI)):
            pi = nextps()
            k.op("pe", lambda e, pi=pi, src=src: e.matmul(PS[pi][0:64, 0:128], lhsT=src, rhs=IDF, start=True, stop=True),
                 reads=[PB, b_const], writes=[PSB[pi]])
            dv(lambda e, pi=pi, dst=dst: e.tensor_copy(out=dst, in_=PS[pi][0:64, 0:128]), reads=[PSB[pi], PB], writes=[PB])
        for src, dst in ((CRW, CR), (CIW, CI)):
            for q in range(4):
                pi = nextps()
                k.begin("pe", reads=[PB, b_const], writes=[PSB[pi]])
                for j in range(4):
                    gb = q * 4 + j
                    ins = nc.tensor.matmul(PS[pi][0:64, j * 128:(j + 1) * 128], lhsT=src[:, gb, :], rhs=IDF,
                                           start=True, stop=True)
                k.end("pe", ins, reads=[PB, b_const], writes=[PSB[pi]])
                dv(lambda e, pi=pi, dst=dst, q=q: e.tensor_copy(
                    out=dst[:, q * 32:(q + 1) * 32, :], in_=PS[pi][0:64, :].rearrange("p (g h) -> p g h", h=16)),
                   reads=[PSB[pi], PB], writes=[PB])
        k.op("act", lambda e: e.activation(out=LDT, in_=LDT, func=AF.Exp), reads=[PB], writes=[PB])
        for blk_ in range(12):
            ada_block(blk_)
        tt(ZR, LR, LDT, ALU.mult); tt(ZI, LI, LDT, ALU.mult)
        ts(ZR, ZR, 1.0 / 16, None, ALU.mult); ts(ZI, ZI, 1.0 / 16, None, ALU.mult)

        def cmul(outr, outi, ar, ai, br, bi):
            tt(T1, ar, br, ALU.mult); tt(T2, ai, bi, ALU.mult); tt(T3, ar, bi, ALU.mult); tt(T4, ai, br, ALU.mult)
            tt(outr, T1, T2, ALU.subtract); tt(outi, T3, T4, ALU.add)

        NTERM = 16
        pd(lambda e: e.memset(ACR, 1.0)); pd(lambda e: e.memset(ACI, 0.0))
        for n in range(NTERM, 1, -1):
            cmul(WR_, WI_, ZR, ZI, ACR, ACI)
            ts(ACR, WR_, 1.0 / n, 1.0, ALU.mult, ALU.add)
            ts(ACI, WI_, 1.0 / n, None, ALU.mult)
        cmul(WR_, WI_, ZR, ZI, ACR, ACI)
        for _ in range(4):
            cmul(ACR, ACI, WR_, WI_, WR_, WI_)
            pd(lambda e: e.scalar_tensor_tensor(out=WR_, in0=WR_, scalar=2.0, in1=ACR, op0=ALU.mult, op1=ALU.add))
            pd(lambda e: e.scalar_tensor_tensor(out=WI_, in0=WI_, scalar=2.0, in1=ACI, op0=ALU.mult, op1=ALU.add))
        tt(T1, LR, LR, ALU.mult); tt(T2, LI, LI, ALU.mult); tt(T1, T1, T2, ALU.add)
        pd(lambda e: e.reciprocal(out=ACR, in_=T1))
        tt(T1, WR_, LR, ALU.mult); tt(T2, WI_, LI, ALU.mult); tt(T1, T1, T2, ALU.add); tt(CFR, T1, ACR, ALU.mult)
        tt(T1, WI_, LR, ALU.mult); tt(T2, WR_, LI, ALU.mult); tt(T1, T1, T2, ALU.subtract); tt(CFI, T1, ACR, ALU.mult)
        ts(ACR, WR_, 1.0, None, ALU.add)
        pd(lambda e: e.tensor_copy(out=ACI, in_=WI_))
        pd(lambda e: e.memset(ER[:, :, 7], 1.0)); pd(lambda e: e.memset(EI[:, :, 7], 0.0))
        pd(lambda e: e.tensor_copy(out=ER[:, :, 8], in_=ACR)); pd(lambda e: e.tensor_copy(out=EI[:, :, 8], in_=ACI))
        for kk in range(9, 16):
            cmul(ER[:, :, kk], EI[:, :, kk], ER[:, :, kk - 1], EI[:, :, kk - 1], ACR, ACI)
        for kq in range(1, 8):
            tt(T1, ER[:, :, 7 + kq], ER[:, :, 7 + kq], ALU.mult); tt(T2, EI[:, :, 7 + kq], EI[:, :, 7 + kq], ALU.mult)
            tt(T1, T1, T2, ALU.add)
            pd(lambda e: e.reciprocal(out=T3, in_=T1))
            tt(ER[:, :, 7 - kq], ER[:, :, 7 + kq], T3, ALU.mult)
            pd(lambda e, kq=kq: e.scalar_tensor_tensor(out=EI[:, :, 7 - kq], in0=EI[:, :, 7 + kq], scalar=-1.0, in1=T3,
                                                       op0=ALU.mult, op1=ALU.mult))
        for s in range(8):
            pd(lambda e, s=s: e.tensor_copy(out=ERZ[:, :, s], in_=ER[:, :, 14 - s]))
            pd(lambda e, s=s: e.tensor_copy(out=EIZ[:, :, s], in_=EI[:, :, 14 - s]))
        for dst, kk in ((A8R, 15), (A4R, 11)):
            pd(lambda e, dst=dst, kk=kk: e.tensor_copy(out=dst, in_=ER[:, :, kk]))
        for dst, kk in ((A8I, 15), (A4I, 11)):
            pd(lambda e, dst=dst, kk=kk: e.tensor_copy(out=dst, in_=EI[:, :, kk]))
        bc16 = lambda a: a.unsqueeze(2).broadcast_to([64, 128, 16])
        tt(BBR, BR, bc16(CFR), ALU.mult); tt(TB, BI, bc16(CFI), ALU.mult); tt(BBR, BBR, TB, ALU.subtract)
        tt(BBI, BI, bc16(CFR), ALU.mult); tt(TB, BR, bc16(CFI), ALU.mult); tt(BBI, BBI, TB, ALU.add)
        GB = 8
        o[0] = o_batch
        ZRb = alloc(64, [GB, 8, 16]); ZIb = alloc(64, [GB, 8, 16]); TZb = alloc(64, [GB, 8, 16])
        CLRb = alloc(64, [GB, 8, 16]); CLIb = alloc(64, [GB, 8, 16])
        WORb = [alloc(64, [GB, 128], BF16) for _ in range(2)]; WOIb = [alloc(64, [GB, 128], BF16) for _ in range(2)]
        WSRb = [alloc(128, [GB, 64], BF16) for _ in range(2)]; WSIb = [alloc(128, [GB, 64], BF16) for _ in range(2)]
        TPb = [alloc(128, [GB, 128], BF16) for _ in range(2)]
        TF1 = alloc(128, [4, 128]); TF2 = alloc(128, [4, 128]); TWb = alloc(64, [GB, 8, 16])
        TZ2b = TZb; PBW = Buf(); PBC = PB
        DTb = [alloc(128, [GB * 16]) for _ in range(2)]; b_dt = [Buf(), Buf()]
        assert o[0] <= U0 + 112 * 1024, o[0]
        b_out = [Buf(), Buf()]
        ada_next = [12]
        for bt in range(128 // GB):
            g0 = bt * GB
            par = bt % 2
            gs_ = slice(g0, g0 + GB)
            k.dma("sp", DTb[par], d_skip[g0 * 16:(g0 + GB) * 16].partition_broadcast(128), writes=[b_dt[par]])
            bs = lambda a: a[:, gs_, :].unsqueeze(2).broadcast_to([64, GB, 8, 16])
            be = lambda a, lo: a[:, gs_, lo:lo + 8].unsqueeze(3).broadcast_to([64, GB, 8, 16])
            tt(ZRb, bs(BBR), be(ERZ, 0), ALU.mult); tt(TZb, bs(BBI), be(EIZ, 0), ALU.mult); tt(ZRb, ZRb, TZb, ALU.subtract)
            tt(ZIb, bs(BBI), be(ERZ, 0), ALU.mult); tt(TZb, bs(BBR), be(EIZ, 0), ALU.mult); tt(ZIb, ZIb, TZb, ALU.add)
            tt(CLRb, bs(CR), be(ER, 0), ALU.mult); tt(TZb, bs(CI), be(EI, 0), ALU.mult); tt(CLRb, CLRb, TZb, ALU.subtract)
            tt(CLIb, bs(CR), be(EI, 0), ALU.mult); tt(TZb, bs(CI), be(ER, 0), ALU.mult); tt(CLIb, CLIb, TZb, ALU.add)
            ts(CLIb, CLIb, -1.0, None, ALU.mult)
            wor4 = WORb[par].rearrange("p g (t h) -> p g t h", h=16); woi4 = WOIb[par].rearrange("p g (t h) -> p g t h", h=16)
            def wout_ops():
                w2 = dict(reads=[b_const, PBW, PBC], writes=[PBW])
                dv(lambda e: e.tensor_tensor(out=TWb, in0=bs(CR), in1=be(ER, 8), op=ALU.mult), **w2)
                dv(lambda e: e.tensor_tensor(out=TZ2b, in0=bs(CI), in1=be(EI, 8), op=ALU.mult), **w2)
                dv(lambda e: e.tensor_tensor(out=wor4, in0=TWb, in1=TZ2b, op=ALU.subtract), reads=[PBW], writes=[PBW, b_out[par]])
                dv(lambda e: e.tensor_tensor(out=TWb, in0=bs(CR), in1=be(EI, 8), op=ALU.mult), **w2)
                dv(lambda e: e.tensor_tensor(out=TZ2b, in0=bs(CI), in1=be(ER, 8), op=ALU.mult), **w2)
                dv(lambda e: e.tensor_tensor(out=TWb, in0=TWb, in1=TZ2b, op=ALU.add), reads=[PBW], writes=[PBW])
                dv(lambda e: e.tensor_scalar(out=woi4, in0=TWb, scalar1=-1.0, scalar2=None, op0=ALU.mult), reads=[PBW],
                   writes=[PBW, b_out[par]])

            for src, dstb in ((ZRb, WSRb[par]), (ZIb, WSIb[par])):
                for q in range(GB // 8):
                    pi = nextps()
                    k.begin("pe", reads=[PB, b_const], writes=[PSB[pi]])
                    for j in range(8):
                        gi = q * 8 + j
                        ins = nc.tensor.matmul(PS[pi][:, j * 64:(j + 1) * 64],
                                               lhsT=src[:, gi, :, :].rearrange("p s h -> p (s h)"),
                                               rhs=IDF[0:64, 0:64], start=True, stop=True)
                    k.end("pe", ins, reads=[PB, b_const], writes=[PSB[pi]])
                    dv(lambda e, pi=pi, dstb=dstb, q=q: e.tensor_copy(
                        out=dstb[:, q * 8:(q + 1) * 8, :], in_=PS[pi][:, :].rearrange("p (g c) -> p g c", c=64)),
                       reads=[PSB[pi]], writes=[b_out[par]])
            toep_pis = []
            for q in range(GB // 4):
                pi = nextps()
                toep_pis.append(pi)
                k.begin("pe", reads=[PB, b_const], writes=[PSB[pi]])
                for j in range(4):
                    gi = q * 4 + j
                    nc.tensor.matmul(PS[pi][:, j * 128:(j + 1) * 128], lhsT=ZRb[:, gi, :, :].rearrange("p s h -> p (s h)"),
                                     rhs=CLRb[:, gi, :, :].rearrange("p s h -> p (s h)"), start=True, stop=False)
                    ins = nc.tensor.matmul(PS[pi][:, j * 128:(j + 1) * 128], lhsT=ZIb[:, gi, :, :].rearrange("p s h -> p (s h)"),
                                           rhs=CLIb[:, gi, :, :].rearrange("p s h -> p (s h)"), start=False, stop=True)
                k.end("pe", ins, reads=[PB, b_const], writes=[PSB[pi]])
            wout_ops()
            for q in range(GB // 4):
                pi = toep_pis[q]
                gq = g0 + q * 4
                dv(lambda e, pi=pi: e.tensor_tensor(out=TF1, in0=PS[pi][:, :].rearrange("p (g c) -> p g c", c=128),
                                                    in1=MASK.unsqueeze(1).broadcast_to([128, 4, 128]), op=ALU.mult),
                   reads=[PSB[pi], b_const, PB], writes=[PB])
                dv(lambda e, gq=gq: e.tensor_tensor(
                    out=TF2[:, :, :].rearrange("p g (t h) -> p g t h", h=16),
                    in0=IDF.rearrange("p (t h) -> p t h", h=16).unsqueeze(1).broadcast_to([128, 4, 8, 16]),
                    in1=DTb[par][:, q * 64:(q + 1) * 64].rearrange("p (g h) -> p g h", h=16).unsqueeze(2).broadcast_to([128, 4, 8, 16]),
                    op=ALU.mult), reads=[PB, b_const, b_dt[par]], writes=[PB])
                dv(lambda e, q=q, par=par: e.tensor_tensor(out=TPb[par][:, q * 4:(q + 1) * 4, :], in0=TF1, in1=TF2, op=ALU.add),
                   reads=[PB], writes=[b_out[par]])
            for q in range(0):
                pi = nextps()
                k.begin("pe", reads=[PB, b_const], writes=[PSB[pi]])
                for j in range(4):
                    gi = q * 4 + j
                    nc.tensor.matmul(PS[pi][:, j * 128:(j + 1) * 128], lhsT=ZRb[:, gi, :, :].rearrange("p s h -> p (s h)"),
                                     rhs=CLRb[:, gi, :, :].rearrange("p s h -> p (s h)"), start=True, stop=False)
                    ins = nc.tensor.matmul(PS[pi][:, j * 128:(j + 1) * 128], lhsT=ZIb[:, gi, :, :].rearrange("p s h -> p (s h)"),
                                           rhs=CLIb[:, gi, :, :].rearrange("p s h -> p (s h)"), start=False, stop=True)
                k.end("pe", ins, reads=[PB, b_const], writes=[PSB[pi]])
                gq = g0 + q * 4
                dv(lambda e, pi=pi: e.tensor_tensor(out=TF1, in0=PS[pi][:, :].rearrange("p (g c) -> p g c", c=128),
                                                    in1=MASK.unsqueeze(1).broadcast_to([128, 4, 128]), op=ALU.mult),
                   reads=[PSB[pi], b_const, PB], writes=[PB])
                dv(lambda e, gq=gq: e.tensor_tensor(
                    out=TF2[:, :, :].rearrange("p g (t h) -> p g t h", h=16),
                    in0=IDF.rearrange("p (t h) -> p t h", h=16).unsqueeze(1).broadcast_to([128, 4, 8, 16]),
                    in1=DTb[par][:, q * 64:(q + 1) * 64].rearrange("p (g h) -> p g h", h=16).unsqueeze(2).broadcast_to([128, 4, 8, 16]),
                    op=ALU.mult), reads=[PB, b_const, b_dt[par]], writes=[PB])
                dv(lambda e, q=q, par=par: e.tensor_tensor(out=TPb[par][:, q * 4:(q + 1) * 4, :], in0=TF1, in1=TF2, op=ALU.add),
                   reads=[PB], writes=[b_out[par]])
            k.dma("sp", WSTR[:, gs_, :], WSRb[par], reads=[b_out[par]])
            k.dma("sp", WSTI[:, gs_, :], WSIb[par], reads=[b_out[par]])
            k.dma("sp", TOEP[:, gs_, :], TPb[par], reads=[b_out[par]])
            k.dma("sp", WOR[:, gs_, :], WORb[par], reads=[b_out[par]])
            k.dma("sp", WOI[:, gs_, :], WOIb[par], reads=[b_out[par]])
            for _ in range(3 if bt < 4 else 2):
                if ada_next[0] < 48:
                    ada_block(ada_next[0])
                    ada_next[0] += 1
        assert ada_next[0] == 48
        k.barrier()

        XN = V(U0, 128, [32, 512], BF16); b_xn = Buf()
        UA = V(U0 + 32768, 128, [16, 512], BF16); b_ua = Buf()
        R2 = U0 + 49152
        ZSA = V(R2, 128, [16, 512], BF16); VB = V(R2 + 16384, 128, [16, 512], BF16)
        MERGED = V(R2 + 32768, 128, [32, 512], BF16)
        b_zsa = [Buf() for _ in range(16)]; b_vb = [Buf() for _ in range(16)]; b_mg = [Buf() for _ in range(32)]
        TOP = U0 + 144 * 1024
        assert TOP + 12 * 1024 <= ARENA_BYTES

        def phase_norm(x_ap, ntok, sample, XN, b_xn, save_halo=False):
            XT = [V(R2, 128, [4096]), V(R2 + 16384, 128, [4096])]; b_xt = [Buf(), Buf()]
            XH = [V(R2 + 32768, 128, [4096], BF16), V(R2 + 40960, 128, [4096], BF16)]; b_xh = [Buf(), Buf()]
            TMPF = [V(TOP, 128, [1024]), V(TOP + 4096, 128, [1024])]; b_tf = [Buf(), Buf()]
            ntt = (ntok + 127) // 128
            tfn = [0]

            def stage_a(tt_):
                rows = min(128, ntok - tt_ * 128)
                s_ = tt_ % 2
                k.dma("sp", XT[s_][0:rows, :], x_ap[tt_ * 128:tt_ * 128 + rows, :], writes=[b_xt[s_]])
                k.op("act", lambda e: e.activation(out=XH[s_][0:rows, :], in_=XT[s_][0:rows, :], func=AF.Square,
                                                   accum_out=SS[0:rows, :]),
                     reads=[b_xt[s_]], writes=[b_xh[s_], b_small])
                k.op("act", lambda e: e.activation(out=RS[0:rows, :], in_=SS[0:rows, :], func=AF.Sqrt, bias=1e-6,
                                                   scale=1.0 / D), reads=[b_small], writes=[b_small])
                dv(lambda e: e.reciprocal(out=RS[0:rows, :], in_=RS[0:rows, :]), reads=[b_small], writes=[b_small])
                k.op("act", lambda e: e.activation(out=XH[s_][0:rows, :], in_=XT[s_][0:rows, :], func=AF.Copy,
                                                   scale=RS[0:rows, 0:1]),
                     reads=[b_xt[s_], b_small], writes=[b_xh[s_]])

            def stage_b(tt_):
                rows = min(128, ntok - tt_ * 128)
                s_ = tt_ % 2
                for q in range(4):
                    pi = nextps()
                    psb = PS[pi][:, :].bitcast(BF16)
                    k.begin("pe", reads=[b_xh[s_], b_const], writes=[PSB[pi]])
                    for j in range(8):
                        kd = q * 8 + j
                        ins = nc.tensor.transpose(psb[:, j * 128:j * 128 + rows], XH[s_][0:rows, kd * 128:(kd + 1) * 128],
                                                  IDB[0:rows, 0:rows])
                    k.end("pe", ins, reads=[b_xh[s_], b_const], writes=[PSB[pi]])
                    ti = tfn[0] % 2
                    tfn[0] += 1
                    pv = psb.rearrange("p (a b) -> p a b", b=128)[:, :, 0:rows]
                    tf = TMPF[ti].rearrange("p (a b) -> p a b", b=128)[:, :, 0:rows]
                    xo = XN[:, q * 8:(q + 1) * 8, tt_ * 128:tt_ * 128 + rows]
                    if not sample:
                        g_ = GSP[:, q * 8:(q + 1) * 8].unsqueeze(2).broadcast_to([128, 8, rows])
                        h_ = SHP[:, q * 8:(q + 1) * 8].unsqueeze(2).broadcast_to([128, 8, rows])
                        dv(lambda e: e.tensor_tensor(out=tf, in0=pv, in1=g_, op=ALU.mult),
                           reads=[PSB[pi], b_mod], writes=[b_tf[ti]])
                        dv(lambda e: e.tensor_tensor(out=xo, in0=tf, in1=h_, op=ALU.add),
                           reads=[b_tf[ti], b_mod], writes=[b_xn], par=True, e="pool")
                    else:
                        pv4 = pv.rearrange("p a (s t) -> p a s t", t=4)
                        tf4 = tf.rearrange("p a (s t) -> p a s t", t=4)
                        xo4 = xo.rearrange("p a (s t) -> p a s t", t=4)
                        g_ = GSS[:, q * 8:(q + 1) * 8, :].unsqueeze(3).broadcast_to([128, 8, 16, 4])
                        h_ = SHS[:, q * 8:(q + 1) * 8, :].unsqueeze(3).broadcast_to([128, 8, 16, 4])
                        dv(lambda e: e.tensor_tensor(out=tf4, in0=pv4, in1=g_, op=ALU.mult),
                           reads=[PSB[pi], b_mod], writes=[b_tf[ti]])
                        dv(lambda e: e.tensor_tensor(out=xo4, in0=tf4, in1=h_, op=ALU.add),
                           reads=[b_tf[ti], b_mod], writes=[b_xn], par=True, e="pool")

            stage_a(0)
            for tt_ in range(ntt):
                if tt_ + 1 < ntt:
                    stage_a(tt_ + 1)
                stage_b(tt_)
            if save_halo:
                dv(lambda e: e.tensor_copy(out=XNH, in_=XN[:, :, ntok - 2:ntok]), reads=[b_xn], writes=[b_prev])

        def evac_copy(i, out, in_, reads, writes):
            if i % 2 == 0:
                return k.op("act", lambda e: e.activation(out=out, in_=in_, func=AF.Copy), reads=reads, writes=writes, par=True)
            return dv(lambda e: e.tensor_copy(out=out, in_=in_), reads=reads, writes=writes, par=True)

        class Seg:
            pass

        SP_ = Seg(); SP_.N = 512; SP_.nseq = 1; SP_.L = 512; SP_.sample = False
        SP_.XN = XN; SP_.UA = UA; SP_.ZSA = ZSA; SP_.VB = VB; SP_.MG = MERGED
        SP_.b_xn = b_xn; SP_.b_ua = b_ua; SP_.b_zsa = b_zsa; SP_.b_vb = b_vb; SP_.b_mg = b_mg
        SS_ = Seg(); SS_.N = 64; SS_.nseq = 16; SS_.L = 4; SS_.sample = True
        SS_.XN = XN_S; SS_.UA = UA_S; SS_.ZSA = ZSA_S; SS_.VB = VB_S; SS_.MG = MERGED_S
        SS_.b_xn = Buf(); SS_.b_ua = Buf(); SS_.b_zsa = [Buf() for _ in range(16)]; SS_.b_vb = [Buf() for _ in range(16)]
        SS_.b_mg = [Buf() for _ in range(32)]

        def seg_streams(segs, rhs_of, lhs_of_m):
            streams = []
            outs = [dict(), dict()]
            ps_s = None
            for m in range(2):
                for sg in segs:
                    if not sg.sample:
                        pi = nextps()
                        o_ = PS[pi][:, 0:sg.N]
                    else:
                        pi = nextps()
                        o_ = PS[pi][:, 0:sg.N]
                    outs[m][sg] = (pi, o_)
                    streams.append((pi, o_, lhs_of_m(m), rhs_of(sg)))
            return streams, outs

        def phase_ua(segs):
            for ctp in range(8):
                streams, outs = seg_streams(segs, (lambda sg: (lambda slot, kd: sg.XN[:, kd, 0:sg.N])),
                                            (lambda m: (lambda slot, kd: slot[:, kd, m * 128:(m + 1) * 128])))
                stream_block(w_in[:, ctp * 256:(ctp + 1) * 256], 32, 256, streams, [sg.b_xn for sg in segs])
                for m in range(2):
                    ct = ctp * 2 + m
                    for sg in segs:
                        pi, o_ = outs[m][sg]
                        evac_copy(ct, sg.UA[:, ct, 0:sg.N], o_, [PSB[pi]], [sg.b_ua])

        CM = V(R2, 64, [128, 128], BF16); b_cm = Buf()
        UT = V(R2 + 32768, 128, [128 * 32], BF16); b_ut = Buf()
        SRE = V(R2 + 40960, 64, [128 * 32]); SIM = V(R2 + 57344, 64, [128 * 32]); b_s = Buf()
        HHR = V(R2 + 73728, 64, [128 * 32], BF16); HHI = V(R2 + 81920, 64, [128 * 32], BF16); b_hh = Buf(); b_hh2 = Buf()
        MATO = R2 + 90112; b_ms = [Buf() for _ in range(3)]; b_my = [Buf() for _ in range(4)]
        SCT_ = [V(TOP + i * 512, 64, [128]) for i in range(6)]; b_sc = Buf(); b_scs = [Buf() for _ in range(6)]
        assert MATO + 6144 <= TOP
        H0R = V(R2 + 40960 + 8192, 64, [128 * 16]); H0I = V(R2 + 57344 + 8192, 64, [128 * 16])
        hpar = [0]

        def s5_subpass(t0, nj, sample, state_only, UA=UA, b_ua=b_ua):
            T = 4 if sample else 8
            slots = [4, 5, 6, 7] if sample else list(range(8))
            ns = len(slots)
            c0 = 64 if sample else 0
            UTv = UT[:, 0:128 * nj].rearrange("p (g j) -> p g j", j=nj)
            SREv = SRE[:, 0:128 * nj].rearrange("p (g j) -> p g j", j=nj)
            SIMv = SIM[:, 0:128 * nj].rearrange("p (g j) -> p g j", j=nj)
            HHRv = HHR[:, 0:128 * nj].rearrange("p (g j) -> p g j", j=nj)
            HHIv = HHI[:, 0:128 * nj].rearrange("p (g j) -> p g j", j=nj)
            CM4 = CM.rearrange("p g (s h) -> p g s h", h=16)
            CMF = CM.rearrange("p g c -> p (g c)")
            CMY = CMF.rearrange("p (t g h) -> p t g h", t=8, g=128, h=16)
            if sample:
                dv(lambda e: e.memset(CM[0:nj, :, 0:64], 0.0), writes=[b_cm])
            for gb in range(16):
                pi = nextps()
                psb = PS[pi][:, :].bitcast(BF16)
                uv = UA[:, gb, t0:t0 + T * nj].rearrange("p (j s) -> p s j", s=T)
                k.begin("pe", reads=[b_ua, b_const], writes=[PSB[pi]])
                for si in range(ns):
                    ins = nc.tensor.transpose(psb[0:nj, si * 128:(si + 1) * 128], uv[:, si if not sample else si, :], IDB)
                k.end("pe", ins, reads=[b_ua, b_const], writes=[PSB[pi]])
                evac_copy(gb, CM4[0:nj, gb * 8:(gb + 1) * 8, slots[0]:slots[0] + ns, :],
                          psb[0:nj, 0:ns * 128].rearrange("p (s g h) -> p g s h", s=ns, g=8, h=16), [PSB[pi]], [b_cm])
            gpb = 1024 // nj
            for q in range(128 // gpb):
                pi = nextps()
                psb = PS[pi][:, :].bitcast(BF16)
                k.begin("pe", reads=[b_cm, b_const], writes=[PSB[pi]])
                for j in range(gpb):
                    g = q * gpb + j
                    ins = nc.tensor.transpose(psb[:, j * nj:(j + 1) * nj], CM[0:nj, g, :], IDB[0:nj, 0:nj])
                k.end("pe", ins, reads=[b_cm, b_const], writes=[PSB[pi]])
                evac_copy(q, UTv[:, q * gpb:(q + 1) * gpb, :], psb[:, 0:gpb * nj].rearrange("p (g j) -> p g j", j=nj),
                          [PSB[pi]], [b_ut])
            for bt in range(16):
                g0 = bt * 8
                mi = bt % 3
                wsr = V(MATO + mi * 2048, 128, [8, 64], BF16); wsi = V(MATO + mi * 2048 + 1024, 128, [8, 64], BF16)
                k.dma("sp", wsr, WSTR[:, g0:g0 + 8, :], writes=[b_ms[mi]])
                k.dma("sp", wsi, WSTI[:, g0:g0 + 8, :], writes=[b_ms[mi]], append=True)
                p1 = nextps(); p2 = nextps()
                k.begin("pe", reads=[b_ms[mi], b_ut], writes=[PSB[p1], PSB[p2]])
                for j in range(8):
                    nc.tensor.matmul(PS[p1][0:64, j * nj:(j + 1) * nj], lhsT=wsr[:, j, :], rhs=UTv[:, g0 + j, :],
                                     start=True, stop=True)
                    ins = nc.tensor.matmul(PS[p2][0:64, j * nj:(j + 1) * nj], lhsT=wsi[:, j, :], rhs=UTv[:, g0 + j, :],
                                           start=True, stop=True)
                k.end("pe", ins, reads=[b_ms[mi], b_ut], writes=[PSB[p1], PSB[p2]])
                evac_copy(0, SREv[:, g0:g0 + 8, :], PS[p1][0:64, 0:8 * nj].rearrange("p (g j) -> p g j", j=nj), [PSB[p1]], [b_s])
                evac_copy(1, SIMv[:, g0:g0 + 8, :], PS[p2][0:64, 0:8 * nj].rearrange("p (g j) -> p g j", j=nj), [PSB[p2]], [b_s])
            t1, t2, t3, t4, t5, t6 = SCT_
            if not sample:
                for j in range(nj):
                    hr, hi = HST[hpar[0]]
                    nr, ni = HST[1 - hpar[0]]
                    if not state_only:
                        k.op("act", lambda e: e.activation(out=HHRv[:, :, j], in_=hr, func=AF.Copy), reads=[b_hst], writes=[b_hh], par=True)
                        k.op("act", lambda e: e.activation(out=HHIv[:, :, j], in_=hi, func=AF.Copy), reads=[b_hst], writes=[b_hh], par=True)
                    bt_ = b_scs
                    rd = [b_hst, b_const]
                    dv(lambda e: e.tensor_tensor(out=t1, in0=A8R, in1=hr, op=ALU.mult), reads=rd, writes=[bt_[0]])
                    dv(lambda e: e.tensor_tensor(out=t2, in0=A8I, in1=hi, op=ALU.mult), reads=rd, writes=[bt_[1]])
                    dv(lambda e: e.tensor_tensor(out=t4, in0=A8R, in1=hi, op=ALU.mult), reads=rd, writes=[bt_[3]])
                    dv(lambda e: e.tensor_tensor(out=t5, in0=A8I, in1=hr, op=ALU.mult), reads=rd, writes=[bt_[4]])
                    dv(lambda e: e.tensor_tensor(out=t3, in0=t1, in1=t2, op=ALU.subtract), reads=[bt_[0], bt_[1]], writes=[bt_[2]])
                    dv(lambda e: e.tensor_tensor(out=t6, in0=t4, in1=t5, op=ALU.add), reads=[bt_[3], bt_[4]], writes=[bt_[5]])
                    dv(lambda e: e.tensor_tensor(out=nr, in0=t3, in1=SREv[:, :, j], op=ALU.add),
                       reads=[bt_[2], b_s], writes=[b_hst], par=True)
                    dv(lambda e: e.tensor_tensor(out=ni, in0=t6, in1=SIMv[:, :, j], op=ALU.add),
                       reads=[bt_[5], b_s], writes=[b_hst], par=True)
                    hpar[0] = 1 - hpar[0]
            else:
                H0Rv = H0R.rearrange("p (g j) -> p g j", j=16); H0Iv = H0I.rearrange("p (g j) -> p g j", j=16)
                dv(lambda e: e.tensor_copy(out=HHRv, in_=H0Rv), reads=[b_hst], writes=[b_hh])
                dv(lambda e: e.tensor_copy(out=HHIv, in_=H0Iv), reads=[b_hst], writes=[b_hh])
                a4r = A4R.unsqueeze(2).broadcast_to([64, 128, 16]); a4i = A4I.unsqueeze(2).broadcast_to([64, 128, 16])
                rw = dict(reads=[b_hst, b_s, b_const], writes=[b_s])
                TAv = V(R2, 64, [128, 16])
                dv(lambda e: e.tensor_tensor(out=TAv, in0=H0Rv, in1=a4r, op=ALU.mult), reads=[b_hst, b_const], writes=[b_cm])
                dv(lambda e: e.tensor_tensor(out=SREv, in0=SREv, in1=TAv, op=ALU.add), reads=[b_cm, b_s], writes=[b_s])
                dv(lambda e: e.tensor_tensor(out=TAv, in0=H0Iv, in1=a4i, op=ALU.mult), reads=[b_hst, b_const, b_s], writes=[b_cm])
                dv(lambda e: e.tensor_tensor(out=SREv, in0=SREv, in1=TAv, op=ALU.subtract), reads=[b_cm, b_s], writes=[b_s])
                dv(lambda e: e.tensor_tensor(out=TAv, in0=H0Iv, in1=a4r, op=ALU.mult), reads=[b_hst, b_const, b_s], writes=[b_cm])
                dv(lambda e: e.tensor_tensor(out=SIMv, in0=SIMv, in1=TAv, op=ALU.add), reads=[b_cm, b_s], writes=[b_s])
                dv(lambda e: e.tensor_tensor(out=TAv, in0=H0Rv, in1=a4i, op=ALU.mult), reads=[b_hst, b_const, b_s], writes=[b_cm])
                dv(lambda e: e.tensor_tensor(out=SIMv, in0=SIMv, in1=TAv, op=ALU.add), reads=[b_cm, b_s], writes=[b_s])
            if state_only:
                return
            HH2 = V(R2 + 73728, 128, [128 * 32], BF16)
            HH2v = HH2[:, 0:128 * nj].rearrange("p (g j) -> p g j", j=nj)
            k.dma("sp", HH2[64:128, 0:128 * nj], HHI[0:64, 0:128 * nj], reads=[b_hh], writes=[b_hh2])
            for bt in range(16):
                g0 = bt * 8
                if sample:
                    mi = 0
                    yo = MATO
                    ybufs = [b_my[0]] + b_ms
                else:
                    mi = bt % 4
                    yo = R2 + 40960 + mi * 6144
                    ybufs = [b_my[mi], b_s]
                tp = V(yo, 128, [8, 128], BF16)
                w2 = V(yo + 2048, 128, [8, 128], BF16)
                k.dma("sp", tp, TOEP[:, g0:g0 + 8, :], writes=ybufs)
                k.dma("sp", w2[0:64], WOR[:, g0:g0 + 8, :], writes=ybufs, append=True)
                k.dma("sp", w2[64:128], WOI[:, g0:g0 + 8, :], writes=ybufs, append=True)
                for hlf in range(2):
                    pi = nextps()
                    k.begin("pe", reads=ybufs + [b_ut, b_hh, b_hh2], writes=[PSB[pi]])
                    for j in range(4):
                        gi = hlf * 4 + j
                        g = g0 + gi
                        nc.tensor.matmul(PS[pi][0:nj, j * 128:(j + 1) * 128], lhsT=UTv[:, g, :], rhs=tp[:, gi, :],
                                         start=True, stop=False)
                        ins = nc.tensor.matmul(PS[pi][0:nj, j * 128 + c0:(j + 1) * 128], lhsT=HH2v[:, g, :],
                                               rhs=w2[:, gi, 0:128 - c0], start=False, stop=True)
                    k.end("pe", ins, reads=ybufs + [b_ut, b_hh, b_hh2], writes=[PSB[pi]])
                    gq = g0 + hlf * 4
                    k.op("act", lambda e: e.activation(out=CMY[0:nj, :, gq:gq + 4, :],
                                                       in_=PS[pi][0:nj, :].rearrange("p (g t h) -> p t g h", g=4, t=8, h=16),
                                                       func=AF.Gelu),
                         reads=[PSB[pi]], writes=[b_cm], par=True)
            for gb in range(16):
                pi = nextps()
                psb = PS[pi][:, :].bitcast(BF16)
                k.begin("pe", reads=[b_cm, b_const], writes=[PSB[pi]])
                for ti in range(ns):
                    t = slots[ti]
                    ins = nc.tensor.transpose(psb[:, ti * nj:(ti + 1) * nj],
                                              CMF[0:nj, t * 2048 + gb * 128:t * 2048 + (gb + 1) * 128], IDB[0:nj, 0:nj])
                k.end("pe", ins, reads=[b_cm, b_const], writes=[PSB[pi]])
                yv = UA[:, gb, t0:t0 + T * nj].rearrange("p (j s) -> p s j", s=T)
                evac_copy(gb, yv, psb[:, 0:ns * nj].rearrange("p (s j) -> p s j", j=nj), [PSB[pi]], [b_ua])

        A0 = R2 + 32768
        SP_.HBT = [V(A0 + i * 2048, 128, [512]) for i in range(2)]
        SP_.FULL = [V(A0 + 4096 + i * 2304, 128, [1, 514]) for i in range(2)]
        SP_.CVO = [V(A0 + 8704 + i * 2048, 128, [512]) for i in range(2)]
        SP_.SZ = [V(A0 + 12800 + i * 2048, 128, [512]) for i in range(2)]
        SS_.HBT = [V(STMP + i * 256, 128, [64]) for i in range(2)]
        SS_.FULL = [V(STMP + 512 + i * 384, 128, [16, 6]) for i in range(2)]
        SS_.CVO = [V(STMP + 1280 + i * 256, 128, [64]) for i in range(2)]
        SS_.SZ = [V(STMP + 1792 + i * 256, 128, [64]) for i in range(2)]
        HBH = [V(A0 + 16896 + i * 8, 128, [2]) for i in range(2)]
        PRH = [V(A0 + 16912 + i * 8, 128, [2]) for i in range(2)]
        b_hbh = [Buf(), Buf()]
        for sg in (SP_, SS_):
            sg.b_hbt = [Buf(), Buf()]; sg.b_full = [Buf(), Buf()]; sg.b_cvo = [Buf(), Buf()]; sg.b_sz = [Buf(), Buf()]

        def phase_A(segs, halo):
            def mm_block(kind, ctp, want_halo):
                streams, outs = seg_streams(segs, (lambda sg: (lambda slot, kd: sg.XN[:, kd, 0:sg.N])),
                                            (lambda m: (lambda slot, kd: slot[:, kd, m * 128:(m + 1) * 128])))
                phs = [None, None]
                if want_halo:
                    for m in range(2):
                        ph = nextps()
                        phs[m] = ph
                        streams.append((ph, PS[ph][:, 0:2], (lambda slot, kd, m=m: slot[:, kd, m * 128:(m + 1) * 128]),
                                        (lambda slot, kd: XNH[:, kd, :])))
                stream_block(w_in[:, kind * 2048 + ctp * 256: kind * 2048 + (ctp + 1) * 256], 32, 256, streams,
                             [sg.b_xn for sg in segs] + [b_prev])
                return outs, phs

            for ctp in range(8):
                cts = [ctp * 2, ctp * 2 + 1]
                outs, _ = mm_block(1, ctp, False)
                for m in range(2):
                    for sg in segs:
                        pi, o_ = outs[m][sg]
                        k.op("act", lambda e: e.activation(out=sg.ZSA[:, cts[m], 0:sg.N], in_=o_, func=AF.Silu),
                             reads=[PSB[pi]], writes=[sg.b_zsa[cts[m]]])
                outs, phs = mm_block(2, ctp, halo)
                for m in range(2):
                    for sg in segs:
                        pi, o_ = outs[m][sg]
                        k.op("act", lambda e: e.activation(out=sg.HBT[m][:, 0:sg.N], in_=o_, func=AF.Copy),
                             reads=[PSB[pi]], writes=[sg.b_hbt[m]])
                    if halo:
                        ph = phs[m]
                        k.op("act", lambda e: e.activation(out=HBH[m], in_=PS[ph][:, 0:2], func=AF.Copy),
                             reads=[PSB[ph]], writes=[b_hbh[m]])
                outs, phs = mm_block(3, ctp, halo)
                for m in range(2):
                    ct = cts[m]
                    if halo:
                        ph = phs[m]
                        dv(lambda e: e.tensor_tensor(out=PRH[m], in0=HBH[m], in1=PS[ph][:, 0:2], op=ALU.mult),
                           reads=[PSB[ph], b_hbh[m]], writes=[b_hbh[m]])
                        dv(lambda e: e.tensor_scalar(out=PREVP[:, ct, :], in0=PRH[m], scalar1=FLAG[:, 0:1], scalar2=None,
                                                     op0=ALU.mult), reads=[b_hbh[m], b_const], writes=[b_prev])
                    for sg in segs:
                        pi, o_ = outs[m][sg]
                        fl = sg.FULL[m]
                        L = sg.L
                        N = sg.N
                        if sg.sample:
                            dv(lambda e: e.tensor_copy(out=fl[:, :, 0:2], in_=PREVS[:, ct, :, :]), reads=[b_prev],
                               writes=[sg.b_full[m]])
                        else:
                            dv(lambda e: e.tensor_copy(out=fl[:, :, 0:2], in_=PREVP[:, ct, :].unsqueeze(1)), reads=[b_prev],
                               writes=[sg.b_full[m]])
                        dv(lambda e: e.tensor_tensor(out=fl[:, :, 2:2 + L], in0=sg.HBT[m][:, 0:N].rearrange("p (s t) -> p s t", t=L),
                                                     in1=o_.rearrange("p (s t) -> p s t", t=L), op=ALU.mult),
                           reads=[PSB[pi], sg.b_hbt[m], sg.b_full[m]], writes=[sg.b_full[m]])
                        if sg.sample:
                            dv(lambda e: e.tensor_copy(out=CONVS[:, ct, :, :], in_=fl[:, :, L:L + 2]), reads=[sg.b_full[m]],
                               writes=[b_prev])
                        else:
                            dv(lambda e: e.tensor_copy(out=PREVP[:, ct, :].unsqueeze(1), in_=fl[:, :, L:L + 2]),
                               reads=[sg.b_full[m]], writes=[b_prev])
                        cv = sg.CVO[m][:, 0:N].rearrange("p (s t) -> p s t", t=L)
                        dv(lambda e: e.tensor_scalar(out=cv, in0=fl[:, :, 0:L], scalar1=CW[:, 0, ct:ct + 1], scalar2=None,
                                                     op0=ALU.mult), reads=[sg.b_full[m], b_const], writes=[sg.b_cvo[m]])
                        dv(lambda e: e.scalar_tensor_tensor(out=cv, in0=fl[:, :, 1:1 + L], scalar=CW[:, 1, ct:ct + 1], in1=cv,
                                                            op0=ALU.mult, op1=ALU.add),
                           reads=[sg.b_full[m], b_const, sg.b_cvo[m]], writes=[sg.b_cvo[m]])
                        dv(lambda e: e.scalar_tensor_tensor(out=cv, in0=fl[:, :, 2:2 + L], scalar=CW[:, 2, ct:ct + 1], in1=cv,
                                                            op0=ALU.mult, op1=ALU.add),
                           reads=[sg.b_full[m], b_const, sg.b_cvo[m]], writes=[sg.b_cvo[m]])
                outs, _ = mm_block(4, ctp, False)
                for m in range(2):
                    for sg in segs:
                        pi, o_ = outs[m][sg]
                        dv(lambda e: e.tensor_tensor(out=sg.CVO[m][:, 0:sg.N], in0=sg.CVO[m][:, 0:sg.N], in1=o_, op=ALU.mult),
                           reads=[PSB[pi], sg.b_cvo[m]], writes=[sg.b_cvo[m]])
                outs, _ = mm_block(5, ctp, False)
                for m in range(2):
                    for sg in segs:
                        pi, o_ = outs[m][sg]
                        k.op("act", lambda e: e.activation(out=sg.SZ[m][:, 0:sg.N], in_=o_, func=AF.Silu),
                             reads=[PSB[pi]], writes=[sg.b_sz[m]])
                        dv(lambda e: e.tensor_tensor(out=sg.VB[:, cts[m], 0:sg.N], in0=sg.CVO[m][:, 0:sg.N],
                                                     in1=sg.SZ[m][:, 0:sg.N], op=ALU.mult),
                           reads=[sg.b_cvo[m], sg.b_sz[m]], writes=[sg.b_vb[cts[m]]])
            for ctp in range(8):
                streams, outs = seg_streams(segs, (lambda sg: (lambda slot, kd: sg.UA[:, kd, 0:sg.N])),
                                            (lambda m: (lambda slot, kd: slot[:, kd, m * 128:(m + 1) * 128])))
                stream_block(w_glu[:, ctp * 256:(ctp + 1) * 256], 16, 256, streams, [sg.b_ua for sg in segs])
                for m in range(2):
                    ct = ctp * 2 + m
                    for sg in segs:
                        pi, o_ = outs[m][sg]
                        N = sg.N
                        k.op("act", lambda e: e.activation(out=sg.SZ[m][:, 0:N], in_=o_, func=AF.Sigmoid,
                                                           bias=BGLU[:, ct:ct + 1]), reads=[PSB[pi], b_const], writes=[sg.b_sz[m]])
                        dv(lambda e: e.tensor_tensor(out=sg.SZ[m][:, 0:N], in0=sg.SZ[m][:, 0:N], in1=sg.UA[:, ct, 0:N], op=ALU.mult),
                           reads=[sg.b_sz[m], sg.b_ua], writes=[sg.b_sz[m]])
                        dv(lambda e: e.tensor_tensor(out=sg.ZSA[:, ct, 0:N], in0=sg.SZ[m][:, 0:N], in1=sg.ZSA[:, ct, 0:N],
                                                     op=ALU.mult), reads=[sg.b_sz[m], sg.b_zsa[ct]], writes=[sg.b_zsa[ct]])

        SP_.BT = [V(TOP + i * 2048, 128, [512]) for i in range(4)]
        SS_.BT = [V(STMP + i * 256, 128, [64]) for i in range(4)]
        for sg in (SP_, SS_):
            sg.b_bt = [Buf() for _ in range(4)]

        def phase_B(segs):
            for dtp in range(16):
                def run_block(src_ap, nk, attr, battr):
                    def rhs_of(sg):
                        t_ = getattr(sg, attr)
                        return lambda slot, kd: t_[:, kd, 0:sg.N]
                    streams, outs = seg_streams(segs, rhs_of, (lambda m: (lambda slot, kd: slot[:, kd, m * 128:(m + 1) * 128])))
                    rb = []
                    for sg in segs:
                        bb = getattr(sg, battr)
                        rb += list(bb) if isinstance(bb, list) else [bb]
                    stream_block(src_ap, nk, 256, streams, rb)
                    return outs

                outs = run_block(w_in[:, 12288 + dtp * 256:12288 + (dtp + 1) * 256], 32, "XN", "b_xn")
                for m in range(2):
                    for sg in segs:
                        pi, o_ = outs[m][sg]
                        k.op("act", lambda e: e.activation(out=sg.BT[m][:, 0:sg.N], in_=o_, func=AF.Sigmoid),
                             reads=[PSB[pi]], writes=[sg.b_bt[m]])
                outs = run_block(w_pa[:, dtp * 256:(dtp + 1) * 256], 16, "ZSA", "b_zsa")
                for m in range(2):
                    for sg in segs:
                        pi, o_ = outs[m][sg]
                        dv(lambda e: e.tensor_tensor(out=sg.BT[m][:, 0:sg.N], in0=sg.BT[m][:, 0:sg.N], in1=o_, op=ALU.mult),
                           reads=[PSB[pi], sg.b_bt[m]], writes=[sg.b_bt[m]])
                outs = run_block(w_in[:, 16384 + dtp * 256:16384 + (dtp + 1) * 256], 32, "XN", "b_xn")
                for m in range(2):
                    for sg in segs:
                        pi, o_ = outs[m][sg]
                        k.op("act", lambda e: e.activation(out=sg.BT[2 + m][:, 0:sg.N], in_=o_, func=AF.Sigmoid),
                             reads=[PSB[pi]], writes=[sg.b_bt[2 + m]])
                outs = run_block(w_pb[:, dtp * 256:(dtp + 1) * 256], 16, "VB", "b_vb")
                for m in range(2):
                    dt_ = dtp * 2 + m
                    for sg in segs:
                        pi, o_ = outs[m][sg]
                        N = sg.N
                        dv(lambda e: e.tensor_tensor(out=sg.BT[2 + m][:, 0:N], in0=sg.BT[2 + m][:, 0:N], in1=o_, op=ALU.mult),
                           reads=[PSB[pi], sg.b_bt[2 + m]], writes=[sg.b_bt[2 + m]])
                        dv(lambda e: e.tensor_tensor(out=sg.MG[:, dt_, 0:N], in0=sg.BT[m][:, 0:N], in1=sg.BT[2 + m][:, 0:N],
                                                     op=ALU.add), reads=[sg.b_bt[m], sg.b_bt[2 + m]], writes=[sg.b_mg[dt_]])

        def phase_C(groups):
            tiles = []
            for (x_ap, y_ap, ntok, sg) in groups:
                for t_ in range((ntok + 127) // 128):
                    rows = min(128, ntok - t_ * 128)
                    tiles.append((x_ap[t_ * 128:t_ * 128 + rows, :], y_ap[t_ * 128:t_ * 128 + rows, :], rows, sg, t_ * 128))
            nt = len(tiles)
            assert nt <= 5
            Hh = [V(U0 + i * 16384, 128, [4096]) for i in range(nt)]; b_h = [Buf() for _ in range(nt)]
            GCB = [V(TOP + i * 1024, 128, [256]) for i in range(2)]; b_gcb = [Buf(), Buf()]
            GCS = [V(TOP + 6400 + i * 1024, 64, [256]) for i in range(2)]; b_gcs = [Buf(), Buf()]
            TMO = [V(TOP + 2048 + i * 1024, 128, [256]) for i in range(2)]; b_tmo = [Buf(), Buf()]
            REP = V(TOP + 4096, 16, [64]); b_rep = Buf()
            GR = [V(TOP + 4352 + i * 1024, 33, [256]) for i in range(2)]; b_gr = [Buf(), Buf()]
            has_p = any(not t[3].sample for t in tiles)
            has_s = any(t[3].sample for t in tiles)
            for i, (xr, yr, rows, sg, c0) in enumerate(tiles):
                k.dma("sp", Hh[i][0:rows, :], xr, writes=[b_h[i]])
            if has_s:
                dv(lambda e: e.tensor_copy(out=REP.rearrange("p (s t) -> p s t", t=4),
                                           in_=IDF[0:16, 0:16].unsqueeze(2).broadcast_to([16, 16, 4])),
                   reads=[b_const], writes=[b_rep])
            mg_bufs = []
            for sg in set(t[3] for t in tiles):
                mg_bufs += list(sg.b_mg)
            for cb in range(16):
                gi = cb % 2
                k.dma("sp", GR[gi], GROWD[:, cb * 256:(cb + 1) * 256], writes=[b_gr[gi]])
                if has_p:
                    pg = nextps()
                    k.op("pe", lambda e: e.matmul(PS[pg][:, 0:256], lhsT=ONES[32:33, :], rhs=GR[gi][32:33, :], start=True, stop=True),
                         reads=[b_const, b_gr[gi]], writes=[PSB[pg]])
                    k.op("act", lambda e: e.activation(out=GCB[gi], in_=PS[pg][:, 0:256], func=AF.Copy),
                         reads=[PSB[pg]], writes=[b_gcb[gi]])
                if has_s:
                    pg2 = nextps()
                    k.op("pe", lambda e: e.matmul(PS[pg2][0:64, 0:256], lhsT=REP, rhs=GR[gi][0:16, :], start=True, stop=True),
                         reads=[b_rep, b_gr[gi]], writes=[PSB[pg2]])
                    k.op("act", lambda e: e.activation(out=GCS[gi], in_=PS[pg2][0:64, 0:256], func=AF.Copy),
                         reads=[PSB[pg2]], writes=[b_gcs[gi]])
                pis = [nextps() for _ in range(nt)]
                streams = []
                for i, (xr, yr, rows, sg, c0) in enumerate(tiles):
                    if sg.sample:
                        streams.append((pis[i], PS[pis[i]][0:128, 0:256],
                                        (lambda slot, kd: MGS_FLAT[:, kd * 64:kd * 64 + 128]),
                                        (lambda slot, kd: slot[:, kd, :])))
                    else:
                        streams.append((pis[i], PS[pis[i]][0:rows, 0:256],
                                        (lambda slot, kd, sg=sg, c0=c0, rows=rows: sg.MG[:, kd, c0:c0 + rows]),
                                        (lambda slot, kd: slot[:, kd, :])))
                stream_block(w_o[:, cb * 256:(cb + 1) * 256], 32, 256, streams, mg_bufs)
                for i, (xr, yr, rows, sg, c0) in enumerate(tiles):
                    pi = pis[i]
                    ti = i % 2
                    gsrc = GCS[gi] if sg.sample else GCB[gi]
                    gbuf = b_gcs[gi] if sg.sample else b_gcb[gi]
                    dv(lambda e: e.tensor_tensor(out=TMO[ti][0:rows, :], in0=PS[pi][0:rows, 0:256], in1=gsrc[0:rows, :],
                                                 op=ALU.mult), reads=[PSB[pi], gbuf], writes=[b_tmo[ti]])
                    hv = Hh[i][0:rows, cb * 256:(cb + 1) * 256]
                    dv(lambda e: e.tensor_tensor(out=hv, in0=hv, in1=TMO[ti][0:rows, :], op=ALU.add),
                       reads=[b_tmo[ti], b_h[i]], writes=[b_h[i]])
            k.barrier()
            JUNK = V(U0 + 81920, 128, [4096], BF16); b_junk = Buf()
            FGB = V(U0 + 90112, 128, [4096]); b_fg = Buf()
            k.dma("sp", FGB, final_g.partition_broadcast(128), writes=[b_fg])
            for i, (xr, yr, rows, sg, c0) in enumerate(tiles):
                hv = Hh[i][0:rows, :]
                SSi = V(TOP + 9216 + i * 8, 128, [1]); RSi = V(TOP + 9220 + i * 8, 128, [1]); b_smi = Buf()
                k.op("act", lambda e: e.activation(out=JUNK[0:rows, :], in_=hv, func=AF.Square, accum_out=SSi[0:rows, :]),
                     reads=[b_h[i]], writes=[b_junk, b_smi])
                k.op("act", lambda e: e.activation(out=RSi[0:rows, :], in_=SSi[0:rows, :], func=AF.Sqrt, bias=1e-6, scale=1.0 / D),
                     reads=[b_smi], writes=[b_smi])
                dv(lambda e: e.reciprocal(out=RSi[0:rows, :], in_=RSi[0:rows, :]), reads=[b_smi], writes=[b_smi])
                dv(lambda e: e.scalar_tensor_tensor(out=hv, in0=hv, scalar=RSi[0:rows, 0:1], in1=FGB[0:rows, :], op0=ALU.mult,
                                                    op1=ALU.mult), reads=[b_h[i], b_smi, b_fg], writes=[b_h[i]])
                k.dma("sp", yr, hv, reads=[b_h[i]])

        for u in range(2):
            phase_norm(xp[u * 512:(u + 1) * 512, :], 512, False, XN, b_xn, save_halo=(u == 1))
            phase_ua([SP_])
            k.barrier()
            for sp_ in range(2):
                s5_subpass(sp_ * 256, 32, False, True)
            k.barrier()
        hr, hi = HST[hpar[0]]
        dv(lambda e: e.tensor_scalar(out=hr, in0=hr, scalar1=FLAG[0:64, 0:1], scalar2=None, op0=ALU.mult), reads=[b_hst, b_const],
           writes=[b_hst])
        dv(lambda e: e.tensor_scalar(out=hi, in0=hi, scalar1=FLAG[0:64, 0:1], scalar2=None, op0=ALU.mult), reads=[b_hst, b_const],
           writes=[b_hst])
        for u in range(2):
            merged_unit = (u == 1)
            segs = [SP_, SS_] if merged_unit else [SP_]
            xa = xm[u * 512:(u + 1) * 512, :]
            phase_norm(xa, 512, False, XN, b_xn)
            if merged_unit:
                k.barrier()
                phase_norm(xs, 64, True, XN_S, SS_.b_xn)
            phase_ua(segs)
            k.barrier()
            for sp_ in range(2):
                s5_subpass(sp_ * 256, 32, False, False)
            k.barrier()
            if merged_unit:
                SRAW = V(R2, 128, [16, 128]); b_sraw = Buf()
                k.dma("sp", SRAW, sssm.rearrange("s g p c -> g s (p c)"), writes=[b_sraw])
                H0Rv = H0R.rearrange("p (g j) -> p g j", j=16); H0Iv = H0I.rearrange("p (g j) -> p g j", j=16)
                for c_, dst in ((0, H0Rv), (1, H0Iv)):
                    for q in range(4):
                        pi = nextps()
                        k.begin("pe", reads=[b_sraw, b_const], writes=[PSB[pi]])
                        for j in range(4):
                            sq = q * 4 + j
                            ins = nc.tensor.matmul(PS[pi][0:64, j * 128:(j + 1) * 128],
                                                   lhsT=SRAW[:, sq, :].rearrange("p (a c) -> p a c", c=2)[:, :, c_], rhs=IDF,
                                                   start=True, stop=True)
                        k.end("pe", ins, reads=[b_sraw, b_const], writes=[PSB[pi]])
                        dv(lambda e: e.tensor_copy(out=dst[:, :, q * 4:(q + 1) * 4].rearrange("p g s -> p s g"),
                                                   in_=PS[pi][0:64, :].rearrange("p (s g) -> p s g", g=128)),
                           reads=[PSB[pi]], writes=[b_hst])
                CVR = V(R2 + 8192, 32, [2048]); b_cvr = Buf()
                k.dma("sp", CVR, sconv.rearrange("s r c -> (s r) c"), writes=[b_cvr])
                for q in range(4):
                    pi = nextps()
                    k.begin("pe", reads=[b_cvr, b_const], writes=[PSB[pi]])
                    for j in range(4):
                        ct = q * 4 + j
                        ins = nc.tensor.matmul(PS[pi][:, j * 32:(j + 1) * 32], lhsT=CVR[:, ct * 128:(ct + 1) * 128],
                                               rhs=IDF[0:32, 0:32], start=True, stop=True)
                    k.end("pe", ins, reads=[b_cvr, b_const], writes=[PSB[pi]])
                    dv(lambda e: e.tensor_copy(out=PREVS[:, q * 4:(q + 1) * 4, :, :].rearrange("p c s r -> p c (s r)"),
                                               in_=PS[pi][:, 0:128].rearrange("p (c x) -> p c x", x=32)),
                       reads=[PSB[pi]], writes=[b_prev])
                k.barrier()
                s5_subpass(0, 16, True, False, UA=UA_S, b_ua=SS_.b_ua)
                k.barrier()
                SREv = SRE[:, 0:128 * 16].rearrange("p (g j) -> p g j", j=16)
                SIMv = SIM[:, 0:128 * 16].rearrange("p (g j) -> p g j", j=16)
                NSS = V(R2, 128, [16, 64, 2])
                b_nss = Buf()
                for c_, src in ((0, SREv), (1, SIMv)):
                    for q in range(2):
                        pi = nextps()
                        k.begin("pe", reads=[b_s, b_const], writes=[PSB[pi]])
                        for j in range(8):
                            sq = q * 8 + j
                            ins = nc.tensor.matmul(PS[pi][:, j * 64:(j + 1) * 64], lhsT=src[:, :, sq], rhs=IDF[0:64, 0:64],
                                                   start=True, stop=True)
                        k.end("pe", ins, reads=[b_s, b_const], writes=[PSB[pi]])
                        dv(lambda e: e.tensor_copy(out=NSS[:, q * 8:(q + 1) * 8, :, c_],
                                                   in_=PS[pi][:, :].rearrange("p (s x) -> p s x", x=64)),
                           reads=[PSB[pi]], writes=[b_nss])
                k.dma("sp", nss.rearrange("s g p c -> g s (p c)"), NSS.rearrange("p s a c -> p s (a c)"), reads=[b_nss])
                k.barrier()
            phase_A(segs, halo=(u == 0))
            k.barrier()
            if merged_unit:
                NCS = V(TOP, 32, [2048]); b_ncs = Buf()
                for q in range(4):
                    pi = nextps()
                    k.begin("pe", reads=[b_prev, b_const], writes=[PSB[pi]])
                    for j in range(4):
                        ct = q * 4 + j
                        ins = nc.tensor.matmul(PS[pi][0:32, j * 128:(j + 1) * 128],
                                               lhsT=CONVS[:, ct, :, :].rearrange("p s r -> p (s r)"), rhs=IDF, start=True, stop=True)
                    k.end("pe", ins, reads=[b_prev, b_const], writes=[PSB[pi]])
                    dv(lambda e: e.tensor_copy(out=NCS[:, q * 512:(q + 1) * 512], in_=PS[pi][0:32, :]), reads=[PSB[pi]],
                       writes=[b_ncs])
                k.dma("sp", ncs.rearrange("s r c -> (s r) c"), NCS, reads=[b_ncs])
                k.barrier()
            phase_B(segs)
            k.barrier()
            if merged_unit:
                phase_C([(xa, yp[u * 512:(u + 1) * 512, :], 512, SP_), (xs, ys, 64, SS_)])
            else:
                phase_C([(xa, yp[u * 512:(u + 1) * 512, :], 512, SP_)])
            k.barrier()
        hr, hi = HST[hpar[0]]
        NSO = V(U0, 128, [64, 2]); b_nso = Buf()
        for c_, src in ((0, hr), (1, hi)):
            pi = nextps()
            k.op("pe", lambda e: e.matmul(PS[pi][:, 0:64], lhsT=src, rhs=IDF[0:64, 0:64], start=True, stop=True),
                 reads=[b_hst, b_const], writes=[PSB[pi]])
            dv(lambda e: e.tensor_copy(out=NSO[:, :, c_], in_=PS[pi][:, 0:64]), reads=[PSB[pi]], writes=[b_nso])
        k.dma("sp", nsp, NSO, reads=[b_nso])
        NCO = V(U0 + 1024, 2, [2048]); b_nco = Buf()
        for q in range(4):
            pi = nextps()
            k.begin("pe", reads=[b_prev, b_const], writes=[PSB[pi]])
            for j in range(4):
                ct = q * 4 + j
                ins = nc.tensor.matmul(PS[pi][0:2, j * 128:(j + 1) * 128], lhsT=PREVP[:, ct, :], rhs=IDF, start=True, stop=True)
            k.end("pe", ins, reads=[b_prev, b_const], writes=[PSB[pi]])
            dv(lambda e: e.tensor_copy(out=NCO[:, q * 512:(q + 1) * 512], in_=PS[pi][0:2, :]), reads=[PSB[pi]], writes=[b_nco])
        k.dma("sp", ncp, NCO, reads=[b_nco])
        k.barrier()
    return nc


_NC = None


def kernel(**inp):
    global _NC
    f = lambda a: np.ascontiguousarray(np.asarray(a, dtype=np.float32))
    x_prompt = f(inp["x_prompt"]); x_sample = f(inp["x_sample"])
    state_ssm = f(inp["state_ssm"]); state_conv = f(inp["state_conv"])
    c_prompt = f(inp["c_prompt"]); c_sample = f(inp["c_sample"])
    if _NC is None:
        _NC = build_nc()
    nc = _NC
    ident = np.eye(128, dtype=np.float32)
    mask = np.zeros((128, 128), np.float32)
    for s in range(8):
        for t in range(s, 8):
            mask[s * 16:(s + 1) * 16, t * 16:(t + 1) * 16] = 1.0
    shared = {
        "ident": ident, "mask": mask,
        "norm_g": f(inp["norm_g"])[0], "w_ada": f(inp["w_ada"])[0], "b_ada": f(inp["b_ada"])[0], "w_in": f(inp["w_in"])[0],
        "lam_re": f(inp["lam_re"])[0], "lam_im": f(inp["lam_im"])[0], "log_dt": f(inp["log_dt"])[0],
        "b_re": f(inp["b_re"])[0], "b_im": f(inp["b_im"])[0], "c_re": f(inp["c_re"])[0], "c_im": f(inp["c_im"])[0],
        "d_skip": f(inp["d_skip"])[0], "w_glu": f(inp["w_glu"])[0], "b_glu": f(inp["b_glu"])[0], "w_pa": f(inp["w_pa"])[0],
        "conv_w": f(inp["conv_w"])[0], "w_pb": f(inp["w_pb"])[0], "w_o": f(inp["w_o"])[0], "final_g": f(inp["final_g"]),
    }
    in_maps = []
    for c in range(8):
        b, half = c // 2, c % 2
        m = dict(shared)
        m["xm"] = np.ascontiguousarray(x_prompt[b, half * 1024:(half + 1) * 1024])
        m["xp"] = np.ascontiguousarray(x_prompt[b, 0:1024])
        m["xs"] = np.ascontiguousarray(x_sample[16 * c:16 * c + 16].reshape(64, D))
        m["cp"] = np.ascontiguousarray(c_prompt[b:b + 1])
        m["cs"] = np.ascontiguousarray(c_sample[16 * c:16 * c + 16])
        m["sssm"] = np.ascontiguousarray(state_ssm[0, 16 * c:16 * c + 16])
        m["sconv"] = np.ascontiguousarray(state_conv[0, 16 * c:16 * c + 16])
        m["flag"] = np.full((128, 1), float(half), np.float32)
        in_maps.append(m)
    res = run_bass_kernel_spmd(nc, in_maps, core_ids=list(range(8)))
    r = res.results
    y_prompt = np.empty((4, 2048, D), np.float32)
    y_sample = np.empty((128, 4, D), np.float32)
    nsp = np.empty((1, 4, 128, 64, 2), np.float32)
    ncp = np.empty((1, 4, 2, 2048), np.float32)
    nss = np.empty((1, 128, 128, 64, 2), np.float32)
    ncs = np.empty((1, 128, 2, 2048), np.float32)
    for c in range(8):
        b, half = c // 2, c % 2
        y_prompt[b, half * 1024:(half + 1) * 1024] = r[c]["yp"]
        y_sample[16 * c:16 * c + 16] = r[c]["ys"].reshape(16, 4, D)
        nss[0, 16 * c:16 * c + 16] = r[c]["nss"]
        ncs[0, 16 * c:16 * c + 16] = r[c]["ncs"]
        if half == 1:
            nsp[0, b] = r[c]["nsp"]
            ncp[0, b] = r[c]["ncp"]
    return (y_prompt, y_sample, nsp, ncp, nss, ncs)
```
